# Optimizing a Trainium2 kernel written in Bass

```python
import math
import jax, jax.numpy as jnp
from jax import lax
import numpy as np

D_MODEL = 1024
BATCH = 8
SEQ = 8192
DEPTH = 1

N_HEADS = 8
HEAD_DIM = 64
ATTN_WIDTH = N_HEADS * HEAD_DIM
MOBA_BLOCK = 256
MOBA_TOPK = 3
Q_CHUNK = 64
ROPE_THETA = 500000.0
ROT_DIM = HEAD_DIM // 4
CONV_GROUPS = 8
CONV_WIDTH = 512
CONV_K = 3
D_FF = 2816
FFN_CONV_K = 3
GATE_WIDTH = 2 * D_MODEL
IN_WIDTH = 3 * ATTN_WIDTH + 3 * CONV_WIDTH + GATE_WIDTH
DEEPNORM_ALPHA = (2.0 * DEPTH) ** 0.25
DEEPNORM_BETA = (8.0 * DEPTH) ** -0.25
LN_EPS = 1e-5

kernel_name = "hybrid_moba_shortconv_convffn_deepnorm"


def layer_norm(x, g, b):
    xf = x.astype(jnp.float32)
    mu = jnp.mean(xf, axis=-1, keepdims=True)
    var = jnp.mean(jnp.square(xf - mu), axis=-1, keepdims=True)
    y = (xf - mu) * lax.rsqrt(var + LN_EPS) * g.astype(jnp.float32) + b.astype(jnp.float32)
    return y.astype(x.dtype)


def causal_dwconv(x, w):
    k = w.shape[0]
    return lax.conv_general_dilated(
        x, w[:, None, :].astype(x.dtype), window_strides=(1,), padding=[(k - 1, 0)],
        dimension_numbers=("NWC", "WIO", "NWC"), feature_group_count=x.shape[-1])


def rope_tables(positions, dtype):
    inv_freq = ROPE_THETA ** (-jnp.arange(0, ROT_DIM, 2, dtype=jnp.float32) / ROT_DIM)
    ang = positions.astype(jnp.float32)[..., None] * inv_freq
    return jnp.cos(ang)[:, :, None, :].astype(dtype), jnp.sin(ang)[:, :, None, :].astype(dtype)


def apply_partial_rope(x, cos, sin):
    xr, xp = x[..., :ROT_DIM], x[..., ROT_DIM:]
    x1, x2 = xr[..., :ROT_DIM // 2], xr[..., ROT_DIM // 2:]
    rot = jnp.concatenate([x1 * cos - x2 * sin, x2 * cos + x1 * sin], axis=-1)
    return jnp.concatenate([rot, xp], axis=-1)


def moba_attention(q, k, v):
    bsz, s, h, dh = q.shape
    nb = -(-s // MOBA_BLOCK)
    s_pad = nb * MOBA_BLOCK
    pad = ((0, 0), (0, s_pad - s), (0, 0), (0, 0))
    k_blk = jnp.pad(k, pad).reshape(bsz, nb, MOBA_BLOCK, h, dh).transpose(0, 3, 1, 2, 4)
    v_blk = jnp.pad(v, pad).reshape(bsz, nb, MOBA_BLOCK, h, dh).transpose(0, 3, 1, 2, 4)
    k_mean = jnp.mean(k_blk.astype(jnp.float32), axis=3)
    topk = min(MOBA_TOPK, nb)
    scale = 1.0 / math.sqrt(dh)
    nqc = s // Q_CHUNK
    q_c = q.reshape(bsz, nqc, Q_CHUNK, h, dh).transpose(0, 1, 3, 2, 4).reshape(bsz * nqc, h, Q_CHUNK, dh)
    b_idx = jnp.repeat(jnp.arange(bsz, dtype=jnp.int32), nqc)
    c_idx = jnp.tile(jnp.arange(nqc, dtype=jnp.int32), bsz)
    head_ix = jnp.arange(h)[:, None, None]
    blk_ids = jnp.arange(nb, dtype=jnp.int32)
    in_blk = jnp.arange(MOBA_BLOCK, dtype=jnp.int32)
    in_chunk = jnp.arange(Q_CHUNK, dtype=jnp.int32)

    def chunk_body(args):
        qc, b, c = args
        kb, vb, km = k_blk[b], v_blk[b], k_mean[b]
        q_pos = c * Q_CHUNK + in_chunk
        own = (c * Q_CHUNK) // MOBA_BLOCK
        s_blk = jnp.einsum("hqd,hnd->hqn", qc.astype(jnp.float32), km)
        s_blk = jnp.where((blk_ids < own)[None, None, :], s_blk, -jnp.inf)
        _, sel = lax.top_k(s_blk, topk)
        valid = sel < own
        k_sel = kb[head_ix, sel]
        v_sel = vb[head_ix, sel]
        s_sel = jnp.einsum("hqd,hqjtd->hqjt", qc, k_sel, preferred_element_type=jnp.float32) * scale
        s_sel = jnp.where(valid[..., None], s_sel, -jnp.inf).reshape(h, Q_CHUNK, topk * MOBA_BLOCK)
        k_own = lax.dynamic_index_in_dim(kb, own, axis=1, keepdims=False)
        v_own = lax.dynamic_index_in_dim(vb, own, axis=1, keepdims=False)
        s_own = jnp.einsum("hqd,htd->hqt", qc, k_own, preferred_element_type=jnp.float32) * scale
        key_pos = own * MOBA_BLOCK + in_blk
        s_own = jnp.where(key_pos[None, None, :] <= q_pos[None, :, None], s_own, -jnp.inf)
        p = jax.nn.softmax(jnp.concatenate([s_sel, s_own], axis=-1), axis=-1).astype(v.dtype)
        p_sel = p[..., :topk * MOBA_BLOCK].reshape(h, Q_CHUNK, topk, MOBA_BLOCK)
        p_own = p[..., topk * MOBA_BLOCK:]
        return (jnp.einsum("hqjt,hqjtd->hqd", p_sel, v_sel)
                + jnp.einsum("hqt,htd->hqd", p_own, v_own))

    o = lax.map(chunk_body, (q_c, b_idx, c_idx))
    return o.reshape(bsz, nqc, h, Q_CHUNK, dh).transpose(0, 1, 3, 2, 4).reshape(bsz, s, h * dh)


def setup_inputs(seed: int = 0) -> dict:
    key = jax.random.key(seed)
    ks = jax.random.split(key, 16)
    f32 = jnp.float32
    nrm = lambda k, shape, s: jax.random.normal(k, shape, f32) * s
    return {
        "x": jax.random.normal(ks[0], (BATCH, SEQ, D_MODEL), f32),
        "positions": jnp.broadcast_to(jnp.arange(SEQ, dtype=jnp.int32), (BATCH, SEQ)),
        "w_in": nrm(ks[1], (DEPTH, D_MODEL, IN_WIDTH), D_MODEL ** -0.5),
        "b_gate": nrm(ks[2], (DEPTH, GATE_WIDTH), 0.1),
        "w_attn_out": nrm(ks[3], (DEPTH, ATTN_WIDTH, D_MODEL), ATTN_WIDTH ** -0.5),
        "conv_w_mix": nrm(ks[4], (DEPTH, CONV_K, CONV_WIDTH), CONV_K ** -0.5),
        "w_conv_out": nrm(ks[5], (DEPTH, CONV_WIDTH, D_MODEL), CONV_WIDTH ** -0.5),
        "w_o": nrm(ks[6], (DEPTH, D_MODEL, D_MODEL), DEEPNORM_BETA * D_MODEL ** -0.5),
        "ln1_g": 1.0 + nrm(ks[7], (DEPTH, D_MODEL), 0.02),
        "ln1_b": nrm(ks[8], (DEPTH, D_MODEL), 0.02),
        "w_up": nrm(ks[9], (DEPTH, D_MODEL, 2 * D_FF), D_MODEL ** -0.5),
        "conv_w_ffn": nrm(ks[10], (DEPTH, FFN_CONV_K, 2 * D_FF), FFN_CONV_K ** -0.5),
        "w_down": nrm(ks[11], (DEPTH, D_FF, D_MODEL), DEEPNORM_BETA * D_FF ** -0.5),
        "ln2_g": 1.0 + nrm(ks[12], (DEPTH, D_MODEL), 0.02),
        "ln2_b": nrm(ks[13], (DEPTH, D_MODEL), 0.02),
    }


def reference(x, positions, w_in, b_gate, w_attn_out, conv_w_mix, w_conv_out, w_o,
              ln1_g, ln1_b, w_up, conv_w_ffn, w_down, ln2_g, ln2_b):
    bsz, s, _ = x.shape
    cos, sin = rope_tables(positions, x.dtype)
    a = ATTN_WIDTH
    c = CONV_WIDTH
    for l in range(DEPTH):
        proj = x @ w_in[l]
        q = proj[..., 0:a].reshape(bsz, s, N_HEADS, HEAD_DIM)
        k = proj[..., a:2 * a].reshape(bsz, s, N_HEADS, HEAD_DIM)
        v = proj[..., 2 * a:3 * a].reshape(bsz, s, N_HEADS, HEAD_DIM)
        off = 3 * a
        gate_b = proj[..., off:off + c]
        gate_c = proj[..., off + c:off + 2 * c]
        h_in = proj[..., off + 2 * c:off + 3 * c]
        gates = jax.nn.sigmoid(proj[..., off + 3 * c:] + b_gate[l])
        g_att, g_cnv = gates[..., :D_MODEL], gates[..., D_MODEL:]
        q = apply_partial_rope(q, cos, sin)
        k = apply_partial_rope(k, cos, sin)
        y_att = moba_attention(q, k, v) @ w_attn_out[l]
        y_cnv = (gate_b * causal_dwconv(gate_c * h_in, conv_w_mix[l])) @ w_conv_out[l]
        mix = (g_att * y_att + g_cnv * y_cnv) @ w_o[l]
        x = layer_norm(DEEPNORM_ALPHA * x + mix, ln1_g[l], ln1_b[l])
        u = causal_dwconv(x @ w_up[l], conv_w_ffn[l])
        f = (jax.nn.silu(u[..., :D_FF]) * u[..., D_FF:]) @ w_down[l]
        x = layer_norm(DEEPNORM_ALPHA * x + f, ln2_g[l], ln2_b[l])
    return x
```

```python
import math
from contextlib import ExitStack

import numpy as np
import concourse.bass as bass
import concourse.mybir as mybir
from concourse.bass_utils import run_bass_kernel_spmd

F32 = mybir.dt.float32
BF16 = mybir.dt.bfloat16
I32 = mybir.dt.int32
AF = mybir.ActivationFunctionType
ALU = mybir.AluOpType
AX = mybir.AxisListType

D = 1024
NH = 8
HD = 64
AW = 512
CW = 512
DFF = 2816
INW = 5120
ALPHA = 2.0 ** 0.25
EPS = 1e-5
NEG = -32768.0
ROPE_THETA = 500000.0
ENGS = ("pe", "act", "dve", "pool", "sp")


class Sem:
    def __init__(self, h):
        self.h = h
        self.n = 0


class Buf:
    __slots__ = ("w", "r", "name", "unread")

    def __init__(self, name=""):
        self.w = []
        self.r = []
        self.name = name
        self.unread = False


class Prog:
    def __init__(self, nc, stack):
        self.nc = nc
        self.stack = stack
        self.q = {e: [] for e in ENGS}
        self.sem = {e: self.new_sem("c_" + e) for e in ENGS}
        self.seen = {e: {} for e in ENGS}
        self.nsem = 0

    def new_sem(self, name):
        return Sem(self.stack.enter_context(self.nc.semaphore(name)))

    def _deps(self, eng, reads, writes, extra):
        best = {}
        evs = list(extra)
        for b in reads:
            evs += b.w
        for b in writes:
            evs += b.w
            evs += b.r
        own = self.sem[eng]
        for (s, v) in evs:
            if s is own and eng in ("pe", "sp"):
                continue
            if best.get(s, 0) < v:
                best[s] = v
        out = []
        seen = self.seen[eng]
        for s, v in best.items():
            if seen.get(s, 0) >= v:
                continue
            seen[s] = v
            out.append((s.h, v))
        return out

    def op(self, eng, fns, reads=(), writes=(), extra=(), acc=False):
        if callable(fns):
            fns = [fns]
        waits = self._deps(eng, reads, writes, extra)
        s = self.sem[eng]
        s.n += 1
        ev = (s, s.n)
        self.q[eng].append((waits, fns, s.h, 1))
        self._record(ev, reads, writes, acc)
        return ev

    @staticmethod
    def _record(ev, reads, writes, acc=False):
        for b in writes:
            if b.name.startswith("bank"):
                if b.unread and not acc:
                    raise RuntimeError("PSUM clobber: %s rewritten before being read" % b.name)
                b.unread = True
        for b in reads:
            b.unread = False
            b.r.append(ev)
            if len(b.r) > 48:
                best = {}
                for (s, v) in b.r:
                    if best.get(s, 0) < v:
                        best[s] = v
                b.r = list(best.items())
        for b in writes:
            b.w = [ev]
            b.r = []

    def dma(self, eng, pairs, sem, reads=(), writes=(), extra=()):
        waits = self._deps(eng, reads, writes, extra)
        fns = [(lambda e, o=o, i=i: e.dma_start(out=o, in_=i)) for (o, i) in pairs]
        sem.n += 16 * len(fns)
        ev = (sem, sem.n)
        self.q[eng].append((waits, fns, sem.h, 16))
        self._record(ev, reads, writes)
        return ev

    def emit(self, final_events=()):
        nc = self.nc
        handles = {"pe": "tensor", "act": "scalar", "dve": "vector", "pool": "gpsimd", "sp": "sync"}
        with nc.Block() as block:
            for e in ENGS:
                q = self.q[e]

                def body(eng, q=q, e=e):
                    for (waits, fns, sh, amt) in q:
                        for (wh, wv) in waits:
                            eng.wait_ge(wh, wv)
                        n = len(fns)
                        for i, f in enumerate(fns):
                            ins = f(eng)
                            if amt == 16 or i == n - 1:
                                ins.then_inc(sh, amt)
                    if e == "sp":
                        for (s, v) in final_events:
                            eng.wait_ge(s.h, v)

                getattr(block, handles[e])(body)
        self.q = {e: [] for e in ENGS}


def build(S, stop_after=None):
    NT = S // 512
    NS = S // 128
    nc = bass.Bass("TRN2", target_bir_lowering=False)
    dram = lambda name, shape, dt, kind: nc.dram_tensor(name, shape, dt, kind=kind).ap()
    xT = dram("xT", [D, S], F32, "ExternalInput")
    posl = dram("posl", [128, NS], I32, "ExternalInput")
    w_in = dram("w_in", [D, INW], F32, "ExternalInput")
    bgl = dram("bgl", [128, 16], F32, "ExternalInput")
    w_att = dram("w_att", [AW, D], F32, "ExternalInput")
    cwm = dram("cwm", [128, 4, 3], F32, "ExternalInput")
    w_cnv = dram("w_cnv", [CW, D], F32, "ExternalInput")
    w_o = dram("w_o", [D, D], F32, "ExternalInput")
    ln1 = dram("ln1", [128, 2, 8], F32, "ExternalInput")
    w_up = dram("w_up", [D, 2 * DFF], F32, "ExternalInput")
    cwf = dram("cwf", [128, 44, 3], F32, "ExternalInput")
    w_dn = dram("w_dn", [DFF, D], F32, "ExternalInput")
    ln2 = dram("ln2", [128, 2, 8], F32, "ExternalInput")
    yT = dram("yT", [D, S], F32, "ExternalOutput")
    oscr = dram("oscr", [AW, S], BF16, "ExternalOutput" if stop_after == 1 else "Internal")
    x1scr = dram("x1scr", [D, S], F32, "ExternalOutput" if stop_after == 2 else "Internal")

    with ExitStack() as st:
        P = Prog(nc, st)
        sb = lambda name, shape, dt: st.enter_context(nc.sbuf_tensor(name, shape, dt))
        pairs = [st.enter_context(nc.psum_tensor("pp%d" % i, [128, 1024], F32)) for i in range(4)]
        banks = [pairs[j // 2][:, (j % 2) * 512:(j % 2 + 1) * 512] for j in range(8)]
        bankb = [Buf("bank%d" % i) for i in range(8)]

        ident = sb("ident", [128, 128], BF16)
        tri = sb("tri", [128, 128], BF16)
        onesb = sb("onesb", [128, 128], BF16)
        onesf = sb("onesf", [128, 64], F32)
        epsc = sb("epsc", [128, 1], F32)
        cB = Buf("consts")

        def mk_consts(g):
            g.memset(onesb[:, :], 1.0)
            g.memset(onesf[:, :], 1.0)
            g.memset(epsc[:, :], EPS)
            g.memset(tri[:, :], 0.0)
            g.affine_select(out=ident[:, :], in_=onesb[:, :], pattern=[[1, 128]], compare_op=ALU.is_equal,
                            fill=0.0, base=0, channel_multiplier=-1)
            return g.affine_select(out=tri[:, :], in_=tri[:, :], pattern=[[1, 128]], compare_op=ALU.is_ge,
                                   fill=NEG, base=0, channel_multiplier=-1)

        P.op("pool", mk_consts, writes=[cB])

        final_events = []

        with ExitStack() as st1:
            sb1 = lambda name, shape, dt: st1.enter_context(nc.sbuf_tensor(name, shape, dt))
            posi = sb1("posi", [128, NS], I32)
            posf = sb1("posf", [128, NS], F32)
            ang = sb1("ang", [128, NS, 8], F32)
            ti = sb1("ti", [128, NS, 8], I32)
            tf = sb1("tf", [128, NS, 8], F32)
            t2 = sb1("t2", [128, NS, 8], F32)
            cosT = sb1("cosT", [128, NS, 8], F32)
            sinT = sb1("sinT", [128, NS, 8], F32)
            s_pos = P.new_sem("s_pos")
            posB, ropeB = Buf(), Buf()
            P.dma("sp", [(posi[:, :], posl)], s_pos, writes=[posB])
            inv_freq = (np.float32(ROPE_THETA) ** (-(np.arange(0, 16, 2, dtype=np.float32)) / np.float32(16))).astype(np.float32)
            TWO_PI = 2.0 * math.pi
            angB, tiB, tfB, t2B = Buf(), Buf(), Buf(), Buf()
            P.op("dve", lambda v: v.tensor_copy(out=posf[:, :], in_=posi[:, :]), reads=[posB], writes=[angB])
            for i in range(8):
                P.op("dve", lambda v, i=i: v.tensor_scalar(out=ang[:, :, i], in0=posf[:, :], scalar1=float(inv_freq[i]),
                                                           scalar2=1.0 / TWO_PI, op0=ALU.mult, op1=ALU.mult),
                     reads=[angB], writes=[angB])
            P.op("dve", lambda v: v.tensor_copy(out=ti[:, :, :], in_=ang[:, :, :]), reads=[angB], writes=[tiB])
            P.op("dve", lambda v: v.tensor_copy(out=tf[:, :, :], in_=ti[:, :, :]), reads=[tiB], writes=[tfB])
            P.op("dve", lambda v: v.tensor_tensor(out=ang[:, :, :], in0=ang[:, :, :], in1=tf[:, :, :], op=ALU.subtract),
                 reads=[tfB, angB], writes=[angB])

            def wrap(buf_ap, bufB):
                P.op("dve", lambda v: v.tensor_scalar(out=t2[:, :, :], in0=buf_ap, scalar1=0.5, scalar2=None, op0=ALU.is_gt),
                     reads=[bufB], writes=[t2B])
                P.op("dve", lambda v: v.tensor_tensor(out=buf_ap, in0=buf_ap, in1=t2[:, :, :], op=ALU.subtract),
                     reads=[t2B, bufB], writes=[bufB])
                P.op("dve", lambda v: v.tensor_scalar(out=t2[:, :, :], in0=buf_ap, scalar1=-0.5, scalar2=None, op0=ALU.is_lt),
                     reads=[bufB], writes=[t2B])
                P.op("dve", lambda v: v.tensor_tensor(out=buf_ap, in0=buf_ap, in1=t2[:, :, :], op=ALU.add),
                     reads=[t2B, bufB], writes=[bufB])

            wrap(ang[:, :, :], angB)
            SC = 6.283185
            P.op("act", lambda a: a.activation(out=sinT[:, :, :], in_=ang[:, :, :], func=AF.Sin, scale=SC),
                 reads=[angB], writes=[ropeB])
            P.op("dve", lambda v: v.tensor_scalar(out=tf[:, :, :], in0=ang[:, :, :], scalar1=0.25, scalar2=None, op0=ALU.add),
                 reads=[angB, tiB], writes=[tfB])
            wrap(tf[:, :, :], tfB)
            P.op("act", lambda a: a.activation(out=cosT[:, :, :], in_=tf[:, :, :], func=AF.Sin, scale=SC),
                 reads=[tfB, ropeB], writes=[ropeB])

            KT = sb1("KT", [96, 4, S], BF16)
            Vt = sb1("Vt", [128, NS, 4, 65], BF16)
            Wq = sb1("Wq", [128, 8, 768], BF16)
            xb = [sb1("xb%d" % i, [128, 8, 512], BF16) for i in range(2)]
            qk = sb1("qk", [128, 4, 8, 64], BF16)
            rt1 = sb1("rt1", [128, 8, 8], F32)
            rt2 = sb1("rt2", [128, 8, 8], F32)
            QT = [sb1("QT%d" % i, [96, 4, 512], BF16) for i in range(2)]
            KM = sb1("KM", [64, 4, 32], BF16)
            scb = sb1("scb", [128, 16, 32], F32)
            maskb = sb1("maskb", [128, 4, 4, 32], F32)
            m8 = sb1("m8", [128, 16, 8], F32)
            ltm = sb1("ltm", [128, 16, 32], F32)
            bpad = sb1("bpad", [128, 4, 4, 96], BF16)
            NPT = 3
            pts = [sb1("pt%d" % i, [128, 1024], BF16) for i in range(NPT)]
            den = sb1("den", [128, 512], F32)
            rec = sb1("rec", [64, 512], F32)
            oT = [sb1("oT%d" % i, [64, 512], BF16) for i in range(2)]

            s_x = [P.new_sem("s_x%d" % i) for i in range(2)]
            s_w1 = P.new_sem("s_w1")
            s_o = [P.new_sem("s_o%d" % i) for i in range(2)]
            xbB = [Buf("xb0"), Buf("xb1")]
            WqB, KTi, VoB, qkB, rtB, KMB = (Buf(n) for n in ("Wq", "KTi", "Vones", "qk", "rt", "KM"))
            KTd = [Buf("KTd%d" % i) for i in range(NT)]
            VB = [Buf("V%d" % i) for i in range(NT)]
            QTd = [Buf("QTd0"), Buf("QTd1")]
            QTb = [Buf("QTb0"), Buf("QTb1")]
            scB, maskB, m8B, ltB, bpB, denB, recB = (Buf(n) for n in ("sc", "maskb", "m8", "lt", "bpad", "den", "rec"))
            ptB = [Buf("pt%d" % i) for i in range(NPT)]
            oTB = [Buf("oT0"), Buf("oT1")]
            SPAIR = [0, 1]
            spB = [bankb[0], bankb[2]]
            OB = 4
            PBK = [5, 6, 7]
            pbk = [0]

            def nb():
                b = PBK[pbk[0] % 3]
                pbk[0] += 1
                return b
            osb = sb1("osb", [65, 512], F32)
            osbB = Buf("osb")

            wsrc = w_in.rearrange("(k p) n -> p k n", p=128)
            xsrc = xT.rearrange("(k p) n -> p k n", p=128)

            def load_w1(g):
                pr = [(Wq[:, :, j * 256:(j + 1) * 256], wsrc[:, :, j * 512 + g * 256: j * 512 + (g + 1) * 256]) for j in range(3)]
                P.dma("pool", pr, s_w1, writes=[WqB])

            def load_x(t):
                P.dma("pool", [(xb[t % 2][:, :, :], xsrc[:, :, t * 512:(t + 1) * 512])], s_x[t % 2], writes=[xbB[t % 2]])

            load_w1(0)
            load_x(0)
            if NT > 1:
                load_x(1)
            CH = 2048
            for c0 in range(0, S, CH):
                n = min(CH, S - c0)
                scr = qk[64:96, :, :, :].rearrange("p a b c -> p (a b c)")[:, 0:n]

                def ind_a(g, scr=scr, c0=c0, n=n):
                    g.memset(scr, 1.0)
                    return g.affine_select(out=scr, in_=scr, pattern=[[1, n]], compare_op=ALU.is_ge, fill=0.0,
                                           base=c0, channel_multiplier=-256)
                P.op("pool", ind_a, writes=[qkB])
                for hl in range(4):
                    P.op("pool", lambda g, scr=scr, c0=c0, n=n, hl=hl: g.affine_select(
                        out=KT[64:96, hl, c0:c0 + n], in_=scr, pattern=[[-1, n]], compare_op=ALU.is_ge, fill=0.0,
                        base=255 - c0, channel_multiplier=256), reads=[qkB], writes=[KTi])
            P.op("pool", lambda g: g.memset(Vt[:, :, :, 64:65], 1.0), writes=[VoB])
            P.op("pool", lambda g: g.memset(bpad[:, :, :, :], 0.0), writes=[bpB])

            def prep_steps(g, t):
                xs, xsB = xb[t % 2], xbB[t % 2]
                QTt, QTdB, QTbB = QT[t % 2], QTd[t % 2], QTb[t % 2]
                for s in range(4):
                    st_i = t * 4 + s
                    vh = 0
                    b1, b2 = nb(), nb()
                    bq, bqB = banks[b1], bankb[b1]
                    bv, bvB = banks[b2], bankb[b2]
                    fns = [(lambda pe, kc=kc, s=s, bq=bq: pe.matmul(bq[:, :], lhsT=xs[:, kc, s * 128:(s + 1) * 128], rhs=Wq[:, kc, 0:512],
                                                              start=(kc == 0), stop=(kc == 7))) for kc in range(8)]
                    P.op("pe", fns, reads=[xsB, WqB], writes=[bqB])
                    ps3 = bq[:, :].rearrange("p (h d) -> p h d", h=8)
                    cosb = cosT[:, st_i:st_i + 1, :].to_broadcast([128, 8, 8])
                    sinb = sinT[:, st_i:st_i + 1, :].to_broadcast([128, 8, 8])
                    P.op("dve", lambda v, ps3=ps3, s=s: v.tensor_copy(out=qk[:, s, :, 16:64], in_=ps3[:, :, 16:64]),
                         reads=[bqB], writes=[qkB])
                    P.op("dve", lambda v, ps3=ps3, cosb=cosb: v.tensor_tensor(out=rt1[:, :, :], in0=ps3[:, :, 0:8], in1=cosb, op=ALU.mult),
                         reads=[bqB, ropeB], writes=[rtB])
                    P.op("dve", lambda v, ps3=ps3, sinb=sinb: v.tensor_tensor(out=rt2[:, :, :], in0=ps3[:, :, 8:16], in1=sinb, op=ALU.mult),
                         reads=[bqB, ropeB], writes=[rtB])
                    P.op("dve", lambda v, s=s: v.tensor_tensor(out=qk[:, s, :, 0:8], in0=rt1[:, :, :], in1=rt2[:, :, :], op=ALU.subtract),
                         reads=[rtB], writes=[qkB, rtB])
                    P.op("dve", lambda v, ps3=ps3, cosb=cosb: v.tensor_tensor(out=rt1[:, :, :], in0=ps3[:, :, 8:16], in1=cosb, op=ALU.mult),
                         reads=[bqB, ropeB], writes=[rtB])
                    P.op("dve", lambda v, ps3=ps3, sinb=sinb: v.tensor_tensor(out=rt2[:, :, :], in0=ps3[:, :, 0:8], in1=sinb, op=ALU.mult),
                         reads=[bqB, ropeB], writes=[rtB])
                    P.op("dve", lambda v, s=s: v.tensor_tensor(out=qk[:, s, :, 8:16], in0=rt1[:, :, :], in1=rt2[:, :, :], op=ALU.add),
                         reads=[rtB], writes=[qkB, rtB])
                    yield
                    fns = [(lambda pe, kc=kc, s=s, vh=vh, bv=bv: pe.matmul(bv[:, vh:vh + 256], lhsT=xs[:, kc, s * 128:(s + 1) * 128],
                                                                     rhs=Wq[:, kc, 512:768], start=(kc == 0), stop=(kc == 7)))
                           for kc in range(8)]
                    P.op("pe", fns, reads=[xsB, WqB], writes=[bvB])
                    P.op("dve", lambda v, vh=vh, st_i=st_i, bv=bv: v.tensor_copy(
                        out=Vt[:, st_i, :, 0:64], in_=bv[:, vh:vh + 256].rearrange("p (h d) -> p h d", h=4)),
                        reads=[bvB], writes=[VB[t]])
                    yield
                if t + 2 < NT:
                    load_x(t + 2)
                for n_i, idx in enumerate(list(range(4, 8)) + list(range(0, 4))):
                    b = nb()
                    fns = [(lambda pe, s=s, idx=idx, b=b: pe.matmul(banks[b][0:64, s * 128:(s + 1) * 128], lhsT=qk[:, s, idx, :],
                                                                    rhs=ident[:, :], start=True, stop=True)) for s in range(4)]
                    P.op("pe", fns, reads=[qkB, cB], writes=[bankb[b]])
                    if idx >= 4:
                        hl = idx - 4
                        P.op("dve", lambda v, b=b, hl=hl: v.tensor_copy(out=KT[0:64, hl, t * 512:(t + 1) * 512], in_=banks[b][0:64, :]),
                             reads=[bankb[b]], writes=[KTd[t]])

                        def kmred(v, b=b, hl=hl):
                            with nc.allow_low_precision("fp32 accumulate, bf16 store of block key sums"):
                                return v.tensor_reduce(out=KM[:, hl, 2 * t:2 * t + 2],
                                                       in_=banks[b][0:64, :].rearrange("p (j k) -> p j k", j=2), axis=AX.X, op=ALU.add)
                        P.op("dve", kmred, reads=[bankb[b]], writes=[KMB])
                    else:
                        hl = idx
                        P.op("dve", lambda v, b=b, hl=hl: v.tensor_copy(out=QTt[0:64, hl, :], in_=banks[b][0:64, :]),
                             reads=[bankb[b]], writes=[QTdB])
                    yield
                BA = nb()
                fns = [(lambda pe, s=s, hl=hl: pe.matmul(banks[BA][:, (s * 4 + hl) * 32:(s * 4 + hl + 1) * 32],
                                                         lhsT=QTt[0:64, hl, s * 128:(s + 1) * 128], rhs=KM[:, hl, :], start=True, stop=True))
                       for s in range(4) for hl in range(4)]
                P.op("pe", fns, reads=[QTdB, KMB], writes=[bankb[BA]])
                P.op("dve", lambda v: v.tensor_tensor(out=scb[:, :, :], in0=banks[BA][:, :].rearrange("p (a j) -> p a j", j=32),
                                                     in1=maskb[:, :, :, :].rearrange("p s h j -> p (s h) j"), op=ALU.add),
                     reads=[bankb[BA], maskB], writes=[scB])
                for a in range(16):
                    P.op("dve", lambda v, a=a: v.max(out=m8[:, a, :], in_=scb[:, a, :]), reads=[scB], writes=[m8B])
                P.op("dve", lambda v: v.tensor_tensor(out=ltm[:, :, :], in0=scb[:, :, :],
                                                     in1=m8[:, :, 2:3].to_broadcast([128, 16, 32]), op=ALU.is_lt),
                     reads=[scB, m8B], writes=[ltB])
                P.op("dve", lambda v: v.tensor_scalar(out=bpad[:, :, :, 64:96].rearrange("p s h j -> p (s h) j"), in0=ltm[:, :, :],
                                                     scalar1=NEG, scalar2=None, op0=ALU.mult), reads=[ltB], writes=[bpB])

                def own_fix(v):
                    v.memset(bpad[:, 0:2, :, 64 + 2 * t:64 + 2 * t + 1], 0.0)
                    return v.memset(bpad[:, 2:4, :, 64 + 2 * t + 1:64 + 2 * t + 2], 0.0)
                P.op("dve", own_fix, reads=[], writes=[bpB])
                if t + 1 < NT:
                    def mask_upd(gg):
                        gg.memset(maskb[:, 0:2, :, 2 * t:2 * t + 2], 0.0)
                        return gg.memset(maskb[:, 2:4, :, 2 * t + 1:2 * t + 3], 0.0)
                    P.op("pool", mask_upd, writes=[maskB])
                yield
                for hl in range(4):
                    b = nb()
                    fns = [(lambda pe, s=s, hl=hl, b=b: pe.matmul(banks[b][0:96, s * 128:(s + 1) * 128], lhsT=bpad[:, s, hl, :],
                                                                  rhs=ident[:, :], start=True, stop=True)) for s in range(4)]
                    P.op("pe", fns, reads=[bpB, cB], writes=[bankb[b]])
                    P.op("dve", lambda v, b=b, hl=hl: v.tensor_copy(out=QTt[64:96, hl, :], in_=banks[b][64:96, :]),
                         reads=[bankb[b]], writes=[QTbB])
                    yield

            pending = []
            sctr = [0]
            pctr = [0]

            def attention(g, t, prep):
                QTt, QTdB, QTbB = QT[t % 2], QTd[t % 2], QTb[t % 2]
                items = []
                for hl in range(4):
                    for pp in range(2 * t):
                        items.append((hl, [2 * pp, 2 * pp + 1]))
                    for c in range(4):
                        items.append((hl, [4 * t + c]))
                nitems = len(items)
                nprep = 21
                slots = {}

                def emit_S(i):
                    hl, kts = items[i]
                    si = sctr[0] % 2
                    sctr[0] += 1
                    slots[i] = si
                    sp = pairs[SPAIR[si]]
                    fns = []
                    for j, kt in enumerate(kts):
                        c = kt - 4 * t
                        q0 = max(c, 0) * 128
                        fns.append(lambda pe, j=j, kt=kt, q0=q0, c=c, hl=hl: pe.matmul(
                            sp[:, j * 512 + q0:(j + 1) * 512], lhsT=KT[0:96, hl, kt * 128:(kt + 1) * 128], rhs=QTt[0:96, hl, q0:512],
                            start=True, stop=(c < 0)))
                        if c >= 0:
                            fns.append(lambda pe, j=j, q0=q0: pe.matmul(
                                sp[:, j * 512 + q0:j * 512 + q0 + 128], lhsT=ident[:, :], rhs=tri[:, :], start=False, stop=True))
                    rd = [KTi, QTdB, QTbB, cB] + [KTd[kt // 4] for kt in kts]
                    P.op("pe", fns, reads=rd, writes=[spB[si]])

                pis = {}

                def emit_E(i):
                    hl, kts = items[i]
                    si = slots.pop(i)
                    sp = pairs[SPAIR[si]]
                    pi = pctr[0] % NPT
                    pctr[0] += 1
                    pis[i] = pi
                    c = kts[0] - 4 * t
                    q0 = max(c, 0) * 128
                    w = len(kts) * 512
                    P.op("act", lambda a: a.activation(out=pts[pi][:, q0:w], in_=sp[:, q0:w], func=AF.Exp, scale=0.125),
                         reads=[spB[si]], writes=[ptB[pi]])

                def emit_P(i):
                    hl, kts = items[i]
                    pi = pis.pop(i)
                    ob = OB
                    c = kts[0] - 4 * t
                    q0 = max(c, 0) * 128
                    nkt = 4 * t + 4
                    for j, kt in enumerate(kts):
                        P.op("pe", lambda pe, j=j, kt=kt: pe.matmul(
                            banks[ob][0:65, q0:512], lhsT=Vt[:, kt, hl, :], rhs=pts[pi][:, j * 512 + q0:(j + 1) * 512],
                            start=(kt == 0), stop=(kt == nkt - 1)),
                            reads=[VB[kt // 4], VoB, ptB[pi]], writes=[bankb[ob]], acc=(kt > 0))
                    if kts[-1] == nkt - 1:
                        def norm(hl=hl, ob=ob):
                            obB = bankb[ob]
                            P.op("dve", lambda v: v.tensor_copy(out=osb[:, :], in_=banks[ob][0:65, :]), reads=[obB], writes=[osbB])
                            BB = nb()
                            P.op("pe", lambda pe: pe.matmul(banks[BB][0:64, :], lhsT=onesf[64:65, 0:64], rhs=osb[64:65, :],
                                                            start=True, stop=True), reads=[osbB, cB], writes=[bankb[BB]])
                            P.op("dve", lambda v: v.reciprocal(out=rec[:, :], in_=banks[BB][0:64, :]), reads=[bankb[BB]], writes=[recB])
                            oi = hl % 2
                            P.op("dve", lambda v: v.tensor_tensor(out=oT[oi][:, :], in0=osb[0:64, :], in1=rec[:, :], op=ALU.mult),
                                 reads=[osbB, recB], writes=[oTB[oi]])
                            hg = g * 4 + hl
                            P.dma("sp", [(oscr[hg * 64:(hg + 1) * 64, t * 512:(t + 1) * 512], oT[oi][:, :])], s_o[oi], reads=[oTB[oi]])
                        pending.append(norm)

                emit_S(0)
                if nitems > 1:
                    emit_S(1)
                done_prep = 0
                for i in range(nitems):
                    emit_E(i)
                    if i + 2 < nitems:
                        emit_S(i + 2)
                    while pending:
                        pending.pop(0)()
                    emit_P(i)
                    if prep is not None:
                        target = min(nprep, (10 * (i + 1) * nprep + 5 * nitems - 1) // (5 * nitems))
                        while done_prep < target:
                            try:
                                next(prep)
                            except StopIteration:
                                prep = None
                                break
                            done_prep += 1
                if prep is not None:
                    for _ in prep:
                        pass

            for g in range(2):
                if g > 0:
                    load_w1(g)
                    load_x(0)
                    if NT > 1:
                        load_x(1)
                P.op("pool", lambda gg: gg.memset(KM[:, :, :], 0.0), writes=[KMB])

                def mask_init(gg):
                    gg.memset(maskb[:, :, :, :], -1e30)
                    return gg.memset(maskb[:, 2:4, :, 0:1], 0.0)
                P.op("pool", mask_init, writes=[maskB])
                for _ in prep_steps(g, 0):
                    pass
                for t in range(NT):
                    attention(g, t, prep_steps(g, t + 1) if t + 1 < NT else None)
                while pending:
                    pending.pop(0)()
            P.emit([(s_o[0], s_o[0].n), (s_o[1], s_o[1].n)])

        if stop_after == 1:
            return nc

        bg_sb = sb("bg_sb", [128, 16], F32)
        cwm_sb = sb("cwm_sb", [128, 4, 3], F32)
        ln1_sb = sb("ln1_sb", [128, 2, 8], F32)
        cwf_sb = sb("cwf_sb", [128, 44, 3], F32)
        ln2_sb = sb("ln2_sb", [128, 2, 8], F32)
        s_vec = P.new_sem("s_vec")
        vecB = Buf("vec")
        P.dma("sp", [(bg_sb[:, :], bgl), (cwm_sb[:, :, :], cwm), (ln1_sb[:, :, :], ln1), (cwf_sb[:, :, :], cwf),
                     (ln2_sb[:, :, :], ln2)], s_vec, writes=[vecB])

        def rearr_w(w):
            return w.rearrange("(k p) n -> p k n", p=128)

        class Ring:
            def __init__(self, idxs):
                self.idxs = idxs
                self.p = 0

            def nxt(self):
                b = self.idxs[self.p % len(self.idxs)]
                self.p += 1
                return b

        def layer_norm_tile(r, rB, S1, S2, gb_sb, stA, stB, stBuf, out_rows, t, s_out):
            S1b, S2b = bankb[S1], bankb[S2]
            P.op("act", lambda a: a.activation(out=stA[:, :], in_=banks[S1][:, :], func=AF.Copy, scale=1.0 / D),
                 reads=[S1b], writes=[stBuf[0]])
            yield
            P.op("dve", lambda v: v.tensor_tensor(out=stB[:, :], in0=stA[:, :], in1=stA[:, :], op=ALU.mult),
                 reads=[stBuf[0]], writes=[stBuf[1]])
            yield
            P.op("dve", lambda v: v.scalar_tensor_tensor(out=stB[:, :], in0=banks[S2][:, :], scalar=1.0 / D, in1=stB[:, :],
                                                          op0=ALU.mult, op1=ALU.subtract),
                 reads=[S2b, stBuf[1]], writes=[stBuf[1]])
            yield
            P.op("act", lambda a: a.activation(out=stB[:, :], in_=stB[:, :], func=AF.Ln, bias=epsc[:, 0:1]),
                 reads=[stBuf[1], cB], writes=[stBuf[1]])
            yield
            P.op("act", lambda a: a.activation(out=stB[:, :], in_=stB[:, :], func=AF.Exp, scale=-0.5),
                 reads=[stBuf[1]], writes=[stBuf[1]])
            yield

            def sub(m):
                P.op("pool", lambda v: v.tensor_tensor(out=r[:, m, :], in0=r[:, m, :], in1=stA[:, :], op=ALU.subtract),
                     reads=[stBuf[0], rB[m]], writes=[rB[m]])

            def mul(m):
                P.op("dve", lambda v: v.tensor_tensor(out=r[:, m, :], in0=r[:, m, :], in1=stB[:, :], op=ALU.mult),
                     reads=[stBuf[1], rB[m]], writes=[rB[m]])

            def aff(m):
                P.op("act", lambda a: a.activation(out=r[:, m, :], in_=r[:, m, :], func=AF.Identity,
                                                   bias=gb_sb[:, 1, m:m + 1], scale=gb_sb[:, 0, m:m + 1]),
                     reads=[rB[m], vecB], writes=[rB[m]])

            def out(m):
                P.dma("sp", [(out_rows[m * 128:(m + 1) * 128, t * 512:(t + 1) * 512], r[:, m, :])], s_out[m % len(s_out)],
                      reads=[rB[m]])
            for k in range(11):
                if k < 8:
                    sub(k)
                if 0 <= k - 1 < 8:
                    mul(k - 1)
                if 0 <= k - 2 < 8:
                    aff(k - 2)
                if 0 <= k - 3 < 8:
                    out(k - 3)
                yield

        ln_gen = [None]

        def ln_step():
            if ln_gen[0] is not None:
                try:
                    next(ln_gen[0])
                except StopIteration:
                    ln_gen[0] = None

        def ln_drain():
            while ln_gen[0] is not None:
                ln_step()

        def ln_stats_chunk(r, rB, m, S1, S2, rbq, rbqB, k):
            rb_t, sq_t = rbq[k % 2]
            rbB, sqB = rbqB[k % 2]
            P.op("act", lambda a: a.activation(out=rb_t[:, :], in_=r[:, m, :], func=AF.Copy), reads=[rB[m]], writes=[rbB])
            P.op("act", lambda a: a.activation(out=sq_t[:, :], in_=r[:, m, :], func=AF.Square), reads=[rB[m]], writes=[sqB])

            def pe_part():
                P.op("pe", lambda pe: pe.matmul(banks[S1][:, :], lhsT=onesb[:, :], rhs=rb_t[:, :], start=(m == 0), stop=(m == 7)),
                     reads=[rbB, cB], writes=[bankb[S1]], acc=(m > 0))
                P.op("pe", lambda pe: pe.matmul(banks[S2][:, :], lhsT=onesb[:, :], rhs=sq_t[:, :], start=(m == 0), stop=(m == 7)),
                     reads=[sqB, cB], writes=[bankb[S2]], acc=(m > 0))
            return pe_part

        with ExitStack() as st2:
            sb2 = lambda name, shape, dt: st2.enter_context(nc.sbuf_tensor(name, shape, dt))
            W2 = sb2("W2", [128, 8, 3584], BF16)
            Wat = sb2("Wat", [128, 4, 1024], BF16)
            Wcv = sb2("Wcv", [128, 4, 1024], BF16)
            Wo = sb2("Wo", [128, 8, 1024], BF16)
            xb2 = [sb2("xb2_%d" % i, [128, 8, 512], BF16) for i in range(2)]
            xfr = [sb2("xfr%d" % i, [128, 512], F32) for i in range(3)]
            oTt = [sb2("oTt%d" % i, [128, 4, 512], BF16) for i in range(2)]
            CHt = [sb2("CH%d" % i, [128, 514], F32) for i in range(2)]
            halo = sb2("halo", [128, 4, 2], F32)
            hS = [sb2("hS%d" % i, [128, 512], F32) for i in range(2)]
            acc = [sb2("acc%d" % i, [128, 512], F32) for i in range(2)]
            cb = sb2("cb", [128, 4, 512], BF16)
            ga = [sb2("ga%d" % i, [128, 512], F32) for i in range(2)]
            gc = [sb2("gc%d" % i, [128, 512], F32) for i in range(2)]
            t1 = [sb2("t1_%d" % i, [128, 512], F32) for i in range(2)]
            t2b = [sb2("t2_%d" % i, [128, 512], F32) for i in range(2)]
            mixin = sb2("mixin", [128, 8, 512], BF16)
            r2 = sb2("r2", [128, 8, 512], F32)
            rbq = [(sb2("rb%d" % i, [128, 512], BF16), sb2("sq%d" % i, [128, 512], BF16)) for i in range(2)]
            stA = sb2("stA", [128, 512], F32)
            stB = sb2("stB", [128, 512], F32)

            s_w2 = [P.new_sem("s_w2_%d" % i) for i in range(10)]
            s_x2 = [P.new_sem("s_x2_%d" % i) for i in range(2)]
            s_xf = [P.new_sem("s_xf%d" % i) for i in range(3)]
            s_ot = [P.new_sem("s_ot%d" % i) for i in range(2)]
            s_x1 = [P.new_sem("s_x1_%d" % i) for i in range(8)]
            W2B = [Buf("W2_%d" % i) for i in range(7)]
            WatB, WcvB, WoB = Buf("Wat"), Buf("Wcv"), Buf("Wo")
            xb2B = [Buf(), Buf()]
            xfrB = [Buf(), Buf(), Buf()]
            oTtB = [Buf(), Buf()]
            CHB = [Buf(), Buf()]
            haloB = Buf()
            hSB = [Buf(), Buf()]
            accB = [Buf(), Buf()]
            cbB = [Buf() for _ in range(4)]
            gaB, gcB, t1B, t2B2 = [Buf(), Buf()], [Buf(), Buf()], [Buf(), Buf()], [Buf(), Buf()]
            mixB = [Buf() for _ in range(8)]
            r2B = [Buf() for _ in range(8)]
            rbqB = [(Buf(), Buf()), (Buf(), Buf())]
            stBuf2 = [Buf(), Buf()]

            for j in range(3):
                P.dma("pool", [(W2[:, :, j * 512:(j + 1) * 512], rearr_w(w_in)[:, :, 1536 + j * 512:1536 + (j + 1) * 512])],
                      s_w2[j], writes=[W2B[j]])
            P.dma("pool", [(Wcv[:, :, :], rearr_w(w_cnv))], s_w2[7], writes=[WcvB])
            P.dma("pool", [(Wat[:, :, :], rearr_w(w_att))], s_w2[8], writes=[WatB])
            for j in range(3, 7):
                P.dma("pool", [(W2[:, :, j * 512:(j + 1) * 512], rearr_w(w_in)[:, :, 1536 + j * 512:1536 + (j + 1) * 512])],
                      s_w2[j], writes=[W2B[j]])
            P.dma("pool", [(Wo[:, :, :], rearr_w(w_o))], s_w2[9], writes=[WoB])
            P.op("pool", lambda gg: gg.memset(halo[:, :, :], 0.0), writes=[haloB])

            def load2(t):
                P.dma("pool", [(xb2[t % 2][:, :, :], rearr_w(xT)[:, :, t * 512:(t + 1) * 512])], s_x2[t % 2],
                      writes=[xb2B[t % 2]])
                P.dma("sp", [(oTt[t % 2][:, :, :], rearr_w(oscr)[:, :, t * 512:(t + 1) * 512])], s_ot[t % 2],
                      writes=[oTtB[t % 2]])

            ring2 = Ring([0, 1, 2, 3, 4, 5])
            xfk = [0]

            def p2_tile(t):
                if t + 1 < NT:
                    load2(t + 1)
                xs, xsB = xb2[t % 2], xb2B[t % 2]
                ot, otB = oTt[t % 2], oTtB[t % 2]

                def proj(col0, wB):
                    b = ring2.nxt()
                    fns = [(lambda pe, kc=kc: pe.matmul(banks[b][:, :], lhsT=W2[:, kc, col0:col0 + 128], rhs=xs[:, kc, :],
                                                        start=(kc == 0), stop=(kc == 7))) for kc in range(8)]
                    P.op("pe", fns, reads=[xsB, wB], writes=[bankb[b]])
                    return b

                for cc in range(4):
                    k = cc % 2
                    bH = proj(1024 + cc * 128, W2B[2])
                    bC = proj(512 + cc * 128, W2B[1])
                    bB = proj(cc * 128, W2B[0])
                    P.op("act", lambda a, bH=bH, k=k: a.activation(out=hS[k][:, :], in_=banks[bH][:, :], func=AF.Copy),
                         reads=[bankb[bH]], writes=[hSB[k]])
                    P.op("pool", lambda gg, k=k, cc=cc: gg.tensor_copy(out=CHt[k][:, 0:2], in_=halo[:, cc, :]),
                         reads=[haloB], writes=[CHB[k]])
                    P.op("dve", lambda v, bC=bC, k=k: v.tensor_tensor(out=CHt[k][:, 2:514], in0=banks[bC][:, :], in1=hS[k][:, :], op=ALU.mult),
                         reads=[bankb[bC], hSB[k]], writes=[CHB[k]], extra=CHB[k].w)
                    P.op("act", lambda a, k=k, cc=cc: a.activation(out=acc[k][:, :], in_=CHt[k][:, 2:514], func=AF.Copy,
                                                                   scale=cwm_sb[:, cc, 2:3]),
                         reads=[CHB[k], vecB], writes=[accB[k]])
                    P.op("dve", lambda v, k=k, cc=cc: v.scalar_tensor_tensor(
                        out=acc[k][:, :], in0=CHt[k][:, 1:513], scalar=cwm_sb[:, cc, 1:2], in1=acc[k][:, :], op0=ALU.mult, op1=ALU.add),
                        reads=[CHB[k], vecB, accB[k]], writes=[accB[k]])
                    P.op("dve", lambda v, k=k, cc=cc: v.scalar_tensor_tensor(
                        out=acc[k][:, :], in0=CHt[k][:, 0:512], scalar=cwm_sb[:, cc, 0:1], in1=acc[k][:, :], op0=ALU.mult, op1=ALU.add),
                        reads=[CHB[k], vecB, accB[k]], writes=[accB[k]])
                    P.op("pool", lambda gg, k=k, cc=cc: gg.tensor_copy(out=halo[:, cc, :], in_=CHt[k][:, 512:514]),
                         reads=[CHB[k]], writes=[haloB])
                    P.op("dve", lambda v, bB=bB, k=k, cc=cc: v.tensor_tensor(out=cb[:, cc, :], in0=banks[bB][:, :], in1=acc[k][:, :], op=ALU.mult),
                         reads=[bankb[bB], accB[k]], writes=[cbB[cc]])
                    ln_step()
                    ln_step()
                for m in range(8):
                    k = m % 2
                    bYc = ring2.nxt()
                    fns = [(lambda pe, kc=kc, m=m, bYc=bYc: pe.matmul(banks[bYc][:, :], lhsT=Wcv[:, kc, m * 128:(m + 1) * 128], rhs=cb[:, kc, :],
                                                        start=(kc == 0), stop=(kc == 3))) for kc in range(4)]
                    P.op("pe", fns, reads=cbB + [WcvB], writes=[bankb[bYc]])
                    bYa = ring2.nxt()
                    fns = [(lambda pe, kc=kc, m=m, bYa=bYa: pe.matmul(banks[bYa][:, :], lhsT=Wat[:, kc, m * 128:(m + 1) * 128], rhs=ot[:, kc, :],
                                                        start=(kc == 0), stop=(kc == 3))) for kc in range(4)]
                    P.op("pe", fns, reads=[otB, WatB], writes=[bankb[bYa]])
                    bGa = proj(1536 + m * 128, W2B[3 + m // 4])
                    bGc = proj(2560 + m * 128, W2B[5 + m // 4])
                    P.op("act", lambda a, bGa=bGa, k=k, m=m: a.activation(out=ga[k][:, :], in_=banks[bGa][:, :], func=AF.Sigmoid,
                                                                          bias=bg_sb[:, m:m + 1]),
                         reads=[bankb[bGa], vecB], writes=[gaB[k]])
                    P.op("act", lambda a, bGc=bGc, k=k, m=m: a.activation(out=gc[k][:, :], in_=banks[bGc][:, :], func=AF.Sigmoid,
                                                                          bias=bg_sb[:, 8 + m:9 + m]),
                         reads=[bankb[bGc], vecB], writes=[gcB[k]])
                    P.op("dve", lambda v, bYa=bYa, k=k: v.tensor_tensor(out=t1[k][:, :], in0=banks[bYa][:, :], in1=ga[k][:, :], op=ALU.mult),
                         reads=[bankb[bYa], gaB[k]], writes=[t1B[k]])
                    P.op("dve", lambda v, bYc=bYc, k=k: v.tensor_tensor(out=t2b[k][:, :], in0=banks[bYc][:, :], in1=gc[k][:, :], op=ALU.mult),
                         reads=[bankb[bYc], gcB[k]], writes=[t2B2[k]])
                    P.op("pool", lambda gg, k=k, m=m: gg.tensor_tensor(out=mixin[:, m, :], in0=t1[k][:, :], in1=t2b[k][:, :], op=ALU.add),
                         reads=[t1B[k], t2B2[k]], writes=[mixB[m]])
                    ln_step()
                    ln_step()
                ln_drain()
                pend = [None]
                for m in range(8):
                    xi = xfk[0] % 3
                    xfk[0] += 1
                    P.dma("sp", [(xfr[xi][:, :], xT[m * 128:(m + 1) * 128, t * 512:(t + 1) * 512])], s_xf[xi], writes=[xfrB[xi]])
                    b = ring2.nxt()
                    fns = [(lambda pe, kc=kc, m=m, b=b: pe.matmul(banks[b][:, :], lhsT=Wo[:, kc, m * 128:(m + 1) * 128], rhs=mixin[:, kc, :],
                                                        start=(kc == 0), stop=(kc == 7))) for kc in range(8)]
                    P.op("pe", fns, reads=mixB + [WoB], writes=[bankb[b]])
                    P.op("dve", lambda v, b=b, xi=xi, m=m: v.scalar_tensor_tensor(
                        out=r2[:, m, :], in0=xfr[xi][:, :], scalar=ALPHA, in1=banks[b][:, :], op0=ALU.mult, op1=ALU.add),
                        reads=[xfrB[xi], bankb[b]], writes=[r2B[m]])
                    if pend[0] is not None:
                        pend[0]()
                    pend[0] = ln_stats_chunk(r2, r2B, m, 6, 7, rbq, rbqB, m)
                pend[0]()
                pend[0] = None
                ln_gen[0] = layer_norm_tile(r2, r2B, 6, 7, ln1_sb, stA, stB, stBuf2, x1scr, t, s_x1)

            load2(0)
            for t in range(NT):
                p2_tile(t)
            ln_drain()
            P.emit([(s, s.n) for s in s_x1])

        if stop_after == 2:
            return nc

        with ExitStack() as st3:
            sb3 = lambda name, shape, dt: st3.enter_context(nc.sbuf_tensor(name, shape, dt))
            Wup = sb3("Wup", [128, 8, 5632], BF16)
            Wdn = sb3("Wdn", [128, 22, 1024], BF16)
            x1b = sb3("x1b", [128, 8, 512], BF16)
            x1r = [sb3("x1r%d" % i, [128, 512], F32) for i in range(3)]
            hid = sb3("hid", [128, 22, 512], BF16)
            accg = [sb3("accg%d" % i, [128, 512], F32) for i in range(2)]
            accv = [sb3("accv%d" % i, [128, 512], F32) for i in range(2)]
            ylast = [sb3("ylast%d" % i, [128, 44, 2], F32) for i in range(2)]
            corr = sb3("corr", [128, 44, 2], F32)
            ctmp = sb3("ctmp", [128, 44], F32)
            r3 = sb3("r3", [128, 8, 512], F32)
            rbq3 = [(sb3("rb3_%d" % i, [128, 512], BF16), sb3("sq3_%d" % i, [128, 512], BF16)) for i in range(2)]
            stA3 = sb3("stA3", [128, 512], F32)
            stB3 = sb3("stB3", [128, 512], F32)

            s_w3 = [P.new_sem("s_w3_%d" % i) for i in range(14)]
            s_x3 = P.new_sem("s_x3")
            s_xr = [P.new_sem("s_xr%d" % i) for i in range(3)]
            s_y = [P.new_sem("s_y%d" % i) for i in range(8)]
            WupB = [Buf() for _ in range(11)]
            WdnB = [Buf() for _ in range(3)]
            x1bB = Buf()
            x1rB = [Buf(), Buf(), Buf()]
            hidB = [Buf() for _ in range(22)]
            accgB, accvB = [Buf(), Buf()], [Buf(), Buf()]
            ylB = [[Buf() for _ in range(44)] for _ in range(2)]
            corrB = Buf()
            r3B = [Buf() for _ in range(8)]
            rbq3B = [(Buf(), Buf()), (Buf(), Buf())]
            stBuf3 = [Buf(), Buf()]

            order = []
            for j in range(11):
                order.append(j)
            seq = [0, 5, 6, 1, 7, 2, 8, 3, 9, 4, 10]
            for j in seq:
                P.dma("pool", [(Wup[:, :, j * 512:(j + 1) * 512], rearr_w(w_up)[:, :, j * 512:(j + 1) * 512])], s_w3[j],
                      writes=[WupB[j]])
            dsp = [(0, 8), (8, 16), (16, 22)]
            for j, (a0, a1) in enumerate(dsp):
                P.dma("pool", [(Wdn[:, a0:a1, :], rearr_w(w_dn)[:, a0:a1, :])], s_w3[11 + j], writes=[WdnB[j]])
            P.op("pool", lambda gg: gg.memset(ylast[0][:, :, :], 0.0), writes=ylB[0])

            def load3(t):
                P.dma("pool", [(x1b[:, :, :], rearr_w(x1scr)[:, :, t * 512:(t + 1) * 512])], s_x3, writes=[x1bB])

            ring3 = Ring([0, 1, 2, 3, 4, 5])
            xrk = [0]

            def wB_for(col):
                return WupB[col // 512]

            def p3_tile(t):
                def up(col0):
                    b = ring3.nxt()
                    fns = [(lambda pe, kc=kc: pe.matmul(banks[b][:, :], lhsT=Wup[:, kc, col0:col0 + 128], rhs=x1b[:, kc, :],
                                                        start=(kc == 0), stop=(kc == 7))) for kc in range(8)]
                    P.op("pe", fns, reads=[x1bB, wB_for(col0)], writes=[bankb[b]])
                    return b

                yl_cur, ylB_cur = ylast[t % 2], ylB[t % 2]
                yl_nxt, ylB_nxt = ylast[(t + 1) % 2], ylB[(t + 1) % 2]
                P.op("pool", lambda gg: gg.tensor_tensor(out=corr[:, :, 0], in0=yl_cur[:, :, 0], in1=cwf_sb[:, :, 0], op=ALU.mult),
                     reads=ylB_cur + [vecB], writes=[corrB])
                P.op("pool", lambda gg: gg.tensor_tensor(out=ctmp[:, :], in0=yl_cur[:, :, 1], in1=cwf_sb[:, :, 1], op=ALU.mult),
                     reads=ylB_cur + [vecB], writes=[corrB])
                P.op("pool", lambda gg: gg.tensor_tensor(out=corr[:, :, 0], in0=corr[:, :, 0], in1=ctmp[:, :], op=ALU.add),
                     reads=[corrB], writes=[corrB])
                P.op("pool", lambda gg: gg.tensor_tensor(out=corr[:, :, 1], in0=yl_cur[:, :, 1], in1=cwf_sb[:, :, 0], op=ALU.mult),
                     reads=ylB_cur + [vecB], writes=[corrB])

                def conv(b, ch, a_t, aB):
                    w0, w1, w2 = cwf_sb[:, ch, 0:1], cwf_sb[:, ch, 1:2], cwf_sb[:, ch, 2:3]
                    P.op("act", lambda a: a.activation(out=a_t[:, :], in_=banks[b][:, :], func=AF.Copy, scale=w2),
                         reads=[bankb[b], vecB], writes=[aB])
                    P.op("act", lambda a: a.activation(out=yl_nxt[:, ch, :], in_=banks[b][:, 510:512], func=AF.Copy),
                         reads=[bankb[b]], writes=[ylB_nxt[ch]])
                    P.op("dve", lambda v: v.scalar_tensor_tensor(out=a_t[:, 1:512], in0=banks[b][:, 0:511], scalar=w1, in1=a_t[:, 1:512],
                                                                  op0=ALU.mult, op1=ALU.add),
                         reads=[bankb[b], vecB, aB], writes=[aB])
                    P.op("dve", lambda v: v.scalar_tensor_tensor(out=a_t[:, 2:512], in0=banks[b][:, 0:510], scalar=w0, in1=a_t[:, 2:512],
                                                                  op0=ALU.mult, op1=ALU.add),
                         reads=[bankb[b], vecB, aB], writes=[aB])
                    P.op("pool", lambda gg: gg.tensor_tensor(out=a_t[:, 0:2], in0=a_t[:, 0:2], in1=corr[:, ch, :], op=ALU.add),
                         reads=[corrB, aB], writes=[aB])

                for i in range(22):
                    k = i % 2
                    bg_ = up(i * 128)
                    bv_ = up(DFF + i * 128)
                    conv(bg_, i, accg[k], accgB[k])
                    conv(bv_, 22 + i, accv[k], accvB[k])
                    P.op("act", lambda a, k=k: a.activation(out=accg[k][:, :], in_=accg[k][:, :], func=AF.Silu),
                         reads=[accgB[k]], writes=[accgB[k]])
                    P.op("pool", lambda gg, k=k, i=i: gg.tensor_tensor(out=hid[:, i, :], in0=accg[k][:, :], in1=accv[k][:, :], op=ALU.mult),
                         reads=[accgB[k], accvB[k]], writes=[hidB[i]])
                    ln_step()
                ln_drain()
                if t + 1 < NT:
                    load3(t + 1)
                pend = [None]
                NF = 16
                dbank = {}
                for m in range(6):
                    b = ring3.nxt()
                    dbank[m] = b
                    fns = [(lambda pe, kc=kc, m=m, b=b: pe.matmul(banks[b][:, :], lhsT=Wdn[:, kc, m * 128:(m + 1) * 128], rhs=hid[:, kc, :],
                                                        start=(kc == 0), stop=False)) for kc in range(NF)]
                    P.op("pe", fns, reads=hidB[:NF] + WdnB, writes=[bankb[b]])
                for m in range(8):
                    xi = xrk[0] % 3
                    xrk[0] += 1
                    P.dma("sp", [(x1r[xi][:, :], x1scr[m * 128:(m + 1) * 128, t * 512:(t + 1) * 512])], s_xr[xi], writes=[x1rB[xi]])
                    if m < 6:
                        b = dbank[m]
                        fns = [(lambda pe, kc=kc, m=m, b=b: pe.matmul(banks[b][:, :], lhsT=Wdn[:, kc, m * 128:(m + 1) * 128], rhs=hid[:, kc, :],
                                                            start=False, stop=(kc == 21))) for kc in range(NF, 22)]
                        P.op("pe", fns, reads=hidB + WdnB, writes=[bankb[b]], acc=True)
                    else:
                        b = ring3.nxt()
                        fns = [(lambda pe, kc=kc, m=m, b=b: pe.matmul(banks[b][:, :], lhsT=Wdn[:, kc, m * 128:(m + 1) * 128], rhs=hid[:, kc, :],
                                                            start=(kc == 0), stop=(kc == 21))) for kc in range(22)]
                        P.op("pe", fns, reads=hidB + WdnB, writes=[bankb[b]])
                    P.op("dve", lambda v, b=b, xi=xi, m=m: v.scalar_tensor_tensor(
                        out=r3[:, m, :], in0=x1r[xi][:, :], scalar=ALPHA, in1=banks[b][:, :], op0=ALU.mult, op1=ALU.add),
                        reads=[x1rB[xi], bankb[b]], writes=[r3B[m]])
                    if pend[0] is not None:
                        pend[0]()
                    pend[0] = ln_stats_chunk(r3, r3B, m, 6, 7, rbq3, rbq3B, m)
                pend[0]()
                pend[0] = None
                ln_gen[0] = layer_norm_tile(r3, r3B, 6, 7, ln2_sb, stA3, stB3, stBuf3, yT, t, s_y)

            load3(0)
            for t in range(NT):
                p3_tile(t)
            ln_drain()
            P.emit([(s, s.n) for s in s_y])
    return nc


def _layout_inputs(inputs, b, S):
    x = np.asarray(inputs["x"])[b]
    pos = np.asarray(inputs["positions"])[b].astype(np.int32)
    f = lambda a: np.ascontiguousarray(np.asarray(a, dtype=np.float32))
    m = {
        "xT": f(x.T),
        "posl": np.ascontiguousarray(pos.reshape(S // 128, 128).T),
        "w_in": f(inputs["w_in"][0]),
        "bgl": f(np.asarray(inputs["b_gate"][0]).reshape(16, 128).T),
        "w_att": f(inputs["w_attn_out"][0]),
        "cwm": f(np.asarray(inputs["conv_w_mix"][0]).reshape(3, 4, 128).transpose(2, 1, 0)),
        "w_cnv": f(inputs["w_conv_out"][0]),
        "w_o": f(inputs["w_o"][0]),
        "ln1": f(np.stack([np.asarray(inputs["ln1_g"][0]).reshape(8, 128).T, np.asarray(inputs["ln1_b"][0]).reshape(8, 128).T], axis=1)),
        "w_up": f(inputs["w_up"][0]),
        "cwf": f(np.asarray(inputs["conv_w_ffn"][0]).reshape(3, 44, 128).transpose(2, 1, 0)),
        "w_dn": f(inputs["w_down"][0]),
        "ln2": f(np.stack([np.asarray(inputs["ln2_g"][0]).reshape(8, 128).T, np.asarray(inputs["ln2_b"][0]).reshape(8, 128).T], axis=1)),
    }
    return m


def kernel(**inputs):
    x = np.asarray(inputs["x"])
    B, S, _ = x.shape
    nc = build(S)
    in_maps = [_layout_inputs(inputs, b, S) for b in range(B)]
    res = run_bass_kernel_spmd(nc, in_maps, core_ids=list(range(B)))
    out = np.stack([np.asarray(res.results[b]["yT"]).T for b in range(B)], axis=0)
    return np.ascontiguousarray(out.astype(np.float32))
```

```python
import math
from contextlib import ExitStack

import numpy as np
import concourse.bass as bass
import concourse.mybir as mybir
from concourse.bass_utils import run_bass_kernel_spmd

F32 = mybir.dt.float32
BF16 = mybir.dt.bfloat16
I32 = mybir.dt.int32
AF = mybir.ActivationFunctionType
ALU = mybir.AluOpType
AX = mybir.AxisListType

D = 1024
NH = 8
HD = 64
AW = 512
CW = 512
DFF = 2816
INW = 5120
ALPHA = 2.0 ** 0.25
EPS = 1e-5
NEG = -32768.0
ROPE_THETA = 500000.0
ENGS = ("pe", "act", "dve", "pool", "sp")


class Sem:
    def __init__(self, h):
        self.h = h
        self.n = 0


class Buf:
    __slots__ = ("w", "r", "name", "unread")

    def __init__(self, name=""):
        self.w = []
        self.r = []
        self.name = name
        self.unread = False


class Prog:
    def __init__(self, nc, stack):
        self.nc = nc
        self.stack = stack
        self.q = {e: [] for e in ENGS}
        self.sem = {e: self.new_sem("c_" + e) for e in ENGS}
        self.seen = {e: {} for e in ENGS}
        self.nsem = 0

    def new_sem(self, name):
        return Sem(self.stack.enter_context(self.nc.semaphore(name)))

    def _deps(self, eng, reads, writes, extra):
        best = {}
        evs = list(extra)
        for b in reads:
            evs += b.w
        for b in writes:
            evs += b.w
            evs += b.r
        own = self.sem[eng]
        for (s, v) in evs:
            if s is own and eng in ("pe", "sp"):
                continue
            if best.get(s, 0) < v:
                best[s] = v
        out = []
        seen = self.seen[eng]
        for s, v in best.items():
            if seen.get(s, 0) >= v:
                continue
            seen[s] = v
            out.append((s.h, v))
        return out

    def op(self, eng, fns, reads=(), writes=(), extra=(), acc=False):
        if callable(fns):
            fns = [fns]
        waits = self._deps(eng, reads, writes, extra)
        s = self.sem[eng]
        s.n += 1
        ev = (s, s.n)
        self.q[eng].append((waits, fns, s.h, 1))
        self._record(ev, reads, writes, acc)
        return ev

    @staticmethod
    def _record(ev, reads, writes, acc=False):
        for b in writes:
            if b.name.startswith("bank"):
                if b.unread and not acc:
                    raise RuntimeError("PSUM clobber: %s rewritten before being read" % b.name)
                b.unread = True
        for b in reads:
            b.unread = False
            b.r.append(ev)
            if len(b.r) > 48:
                best = {}
                for (s, v) in b.r:
                    if best.get(s, 0) < v:
                        best[s] = v
                b.r = list(best.items())
        for b in writes:
            b.w = [ev]
            b.r = []

    def dma(self, eng, pairs, sem, reads=(), writes=(), extra=()):
        waits = self._deps(eng, reads, writes, extra)
        fns = [(lambda e, o=o, i=i: e.dma_start(out=o, in_=i)) for (o, i) in pairs]
        sem.n += 16 * len(fns)
        ev = (sem, sem.n)
        self.q[eng].append((waits, fns, sem.h, 16))
        self._record(ev, reads, writes)
        return ev

    def emit(self, final_events=()):
        nc = self.nc
        handles = {"pe": "tensor", "act": "scalar", "dve": "vector", "pool": "gpsimd", "sp": "sync"}
        with nc.Block() as block:
            for e in ENGS:
                q = self.q[e]

                def body(eng, q=q, e=e):
                    for (waits, fns, sh, amt) in q:
                        for (wh, wv) in waits:
                            eng.wait_ge(wh, wv)
                        n = len(fns)
                        for i, f in enumerate(fns):
                            ins = f(eng)
                            if amt == 16 or i == n - 1:
                                ins.then_inc(sh, amt)
                    if e == "sp":
                        for (s, v) in final_events:
                            eng.wait_ge(s.h, v)

                getattr(block, handles[e])(body)
        self.q = {e: [] for e in ENGS}


def build(S, stop_after=None):
    NT = S // 512
    NS = S // 128
    nc = bass.Bass("TRN2", target_bir_lowering=False)
    dram = lambda name, shape, dt, kind: nc.dram_tensor(name, shape, dt, kind=kind).ap()
    xT = dram("xT", [D, S], F32, "ExternalInput")
    posl = dram("posl", [128, NS], I32, "ExternalInput")
    w_in = dram("w_in", [D, INW], F32, "ExternalInput")
    bgl = dram("bgl", [128, 16], F32, "ExternalInput")
    w_att = dram("w_att", [AW, D], F32, "ExternalInput")
    cwm = dram("cwm", [128, 4, 3], F32, "ExternalInput")
    w_cnv = dram("w_cnv", [CW, D], F32, "ExternalInput")
    w_o = dram("w_o", [D, D], F32, "ExternalInput")
    ln1 = dram("ln1", [128, 2, 8], F32, "ExternalInput")
    w_up = dram("w_up", [D, 2 * DFF], F32, "ExternalInput")
    cwf = dram("cwf", [128, 44, 3], F32, "ExternalInput")
    w_dn = dram("w_dn", [DFF, D], F32, "ExternalInput")
    ln2 = dram("ln2", [128, 2, 8], F32, "ExternalInput")
    yT = dram("yT", [D, S], F32, "ExternalOutput")
    oscr = dram("oscr", [AW, S], BF16, "ExternalOutput" if stop_after == 1 else "Internal")
    x1scr = dram("x1scr", [D, S], F32, "ExternalOutput" if stop_after == 2 else "Internal")

    with ExitStack() as st:
        P = Prog(nc, st)
        sb = lambda name, shape, dt: st.enter_context(nc.sbuf_tensor(name, shape, dt))
        pairs = [st.enter_context(nc.psum_tensor("pp%d" % i, [128, 1024], F32)) for i in range(4)]
        banks = [pairs[j // 2][:, (j % 2) * 512:(j % 2 + 1) * 512] for j in range(8)]
        bankb = [Buf("bank%d" % i) for i in range(8)]

        ident = sb("ident", [128, 128], BF16)
        tri = sb("tri", [128, 128], BF16)
        onesb = sb("onesb", [128, 128], BF16)
        onesf = sb("onesf", [128, 64], F32)
        epsc = sb("epsc", [128, 1], F32)
        cB = Buf("consts")

        def mk_consts(g):
            g.memset(onesb[:, :], 1.0)
            g.memset(onesf[:, :], 1.0)
            g.memset(epsc[:, :], EPS)
            g.memset(tri[:, :], 0.0)
            g.affine_select(out=ident[:, :], in_=onesb[:, :], pattern=[[1, 128]], compare_op=ALU.is_equal,
                            fill=0.0, base=0, channel_multiplier=-1)
            return g.affine_select(out=tri[:, :], in_=tri[:, :], pattern=[[1, 128]], compare_op=ALU.is_ge,
                                   fill=NEG, base=0, channel_multiplier=-1)

        P.op("pool", mk_consts, writes=[cB])

        final_events = []

        with ExitStack() as st1:
            sb1 = lambda name, shape, dt: st1.enter_context(nc.sbuf_tensor(name, shape, dt))
            posi = sb1("posi", [128, NS], I32)
            posf = sb1("posf", [128, NS], F32)
            ang = sb1("ang", [128, NS, 8], F32)
            ti = sb1("ti", [128, NS, 8], I32)
            tf = sb1("tf", [128, NS, 8], F32)
            t2 = sb1("t2", [128, NS, 8], F32)
            cosT = sb1("cosT", [128, NS, 8], F32)
            sinT = sb1("sinT", [128, NS, 8], F32)
            s_pos = P.new_sem("s_pos")
            posB, ropeB = Buf(), Buf()
            P.dma("sp", [(posi[:, :], posl)], s_pos, writes=[posB])
            inv_freq = (np.float32(ROPE_THETA) ** (-(np.arange(0, 16, 2, dtype=np.float32)) / np.float32(16))).astype(np.float32)
            TWO_PI = 2.0 * math.pi
            angB, tiB, tfB, t2B = Buf(), Buf(), Buf(), Buf()
            P.op("dve", lambda v: v.tensor_copy(out=posf[:, :], in_=posi[:, :]), reads=[posB], writes=[angB])
            for i in range(8):
                P.op("dve", lambda v, i=i: v.tensor_scalar(out=ang[:, :, i], in0=posf[:, :], scalar1=float(inv_freq[i]),
                                                           scalar2=1.0 / TWO_PI, op0=ALU.mult, op1=ALU.mult),
                     reads=[angB], writes=[angB])
            P.op("dve", lambda v: v.tensor_copy(out=ti[:, :, :], in_=ang[:, :, :]), reads=[angB], writes=[tiB])
            P.op("dve", lambda v: v.tensor_copy(out=tf[:, :, :], in_=ti[:, :, :]), reads=[tiB], writes=[tfB])
            P.op("dve", lambda v: v.tensor_tensor(out=ang[:, :, :], in0=ang[:, :, :], in1=tf[:, :, :], op=ALU.subtract),
                 reads=[tfB, angB], writes=[angB])

            def wrap(buf_ap, bufB):
                P.op("dve", lambda v: v.tensor_scalar(out=t2[:, :, :], in0=buf_ap, scalar1=0.5, scalar2=None, op0=ALU.is_gt),
                     reads=[bufB], writes=[t2B])
                P.op("dve", lambda v: v.tensor_tensor(out=buf_ap, in0=buf_ap, in1=t2[:, :, :], op=ALU.subtract),
                     reads=[t2B, bufB], writes=[bufB])
                P.op("dve", lambda v: v.tensor_scalar(out=t2[:, :, :], in0=buf_ap, scalar1=-0.5, scalar2=None, op0=ALU.is_lt),
                     reads=[bufB], writes=[t2B])
                P.op("dve", lambda v: v.tensor_tensor(out=buf_ap, in0=buf_ap, in1=t2[:, :, :], op=ALU.add),
                     reads=[t2B, bufB], writes=[bufB])

            wrap(ang[:, :, :], angB)
            SC = 6.283185
            P.op("act", lambda a: a.activation(out=sinT[:, :, :], in_=ang[:, :, :], func=AF.Sin, scale=SC),
                 reads=[angB], writes=[ropeB])
            P.op("dve", lambda v: v.tensor_scalar(out=tf[:, :, :], in0=ang[:, :, :], scalar1=0.25, scalar2=None, op0=ALU.add),
                 reads=[angB, tiB], writes=[tfB])
            wrap(tf[:, :, :], tfB)
            P.op("act", lambda a: a.activation(out=cosT[:, :, :], in_=tf[:, :, :], func=AF.Sin, scale=SC),
                 reads=[tfB, ropeB], writes=[ropeB])

            KT = sb1("KT", [96, 4, S], BF16)
            Vt = sb1("Vt", [128, NS, 4, 65], BF16)
            Wq = sb1("Wq", [128, 8, 768], BF16)
            xb = [sb1("xb%d" % i, [128, 8, 512], BF16) for i in range(2)]
            qk = sb1("qk", [128, 4, 8, 64], BF16)
            rt1 = sb1("rt1", [128, 8, 8], F32)
            rt2 = sb1("rt2", [128, 8, 8], F32)
            QT = [sb1("QT%d" % i, [96, 4, 512], BF16) for i in range(2)]
            KM = sb1("KM", [64, 4, 32], BF16)
            scb = sb1("scb", [128, 16, 32], F32)
            maskb = sb1("maskb", [128, 4, 4, 32], F32)
            m8 = sb1("m8", [128, 16, 8], F32)
            ltm = sb1("ltm", [128, 16, 32], F32)
            bpad = sb1("bpad", [128, 4, 4, 96], BF16)
            NPT = 3
            pts = [sb1("pt%d" % i, [128, 1024], BF16) for i in range(NPT)]
            den = sb1("den", [128, 512], F32)
            rec = sb1("rec", [64, 512], F32)
            oT = [sb1("oT%d" % i, [64, 512], BF16) for i in range(2)]

            s_x = [P.new_sem("s_x%d" % i) for i in range(2)]
            s_w1 = P.new_sem("s_w1")
            s_o = [P.new_sem("s_o%d" % i) for i in range(2)]
            xbB = [Buf("xb0"), Buf("xb1")]
            WqB, KTi, VoB, qkB, rtB, KMB = (Buf(n) for n in ("Wq", "KTi", "Vones", "qk", "rt", "KM"))
            KTd = [Buf("KTd%d" % i) for i in range(NT)]
            VB = [Buf("V%d" % i) for i in range(NT)]
            QTd = [Buf("QTd0"), Buf("QTd1")]
            QTb = [Buf("QTb0"), Buf("QTb1")]
            scB, maskB, m8B, ltB, bpB, denB, recB = (Buf(n) for n in ("sc", "maskb", "m8", "lt", "bpad", "den", "rec"))
            ptB = [Buf("pt%d" % i) for i in range(NPT)]
            oTB = [Buf("oT0"), Buf("oT1")]
            SPAIR = [0, 1]
            spB = [bankb[0], bankb[2]]
            OB = 4
            PBK = [5, 6, 7]
            pbk = [0]

            def nb():
                b = PBK[pbk[0] % 3]
                pbk[0] += 1
                return b
            osb = sb1("osb", [65, 512], F32)
            osbB = Buf("osb")

            wsrc = w_in.rearrange("(k p) n -> p k n", p=128)
            xsrc = xT.rearrange("(k p) n -> p k n", p=128)

            def load_w1(g):
                pr = [(Wq[:, :, j * 256:(j + 1) * 256], wsrc[:, :, j * 512 + g * 256: j * 512 + (g + 1) * 256]) for j in range(3)]
                P.dma("pool", pr, s_w1, writes=[WqB])

            def load_x(t):
                P.dma("pool", [(xb[t % 2][:, :, :], xsrc[:, :, t * 512:(t + 1) * 512])], s_x[t % 2], writes=[xbB[t % 2]])

            load_w1(0)
            load_x(0)
            if NT > 1:
                load_x(1)
            CH = 2048
            for c0 in range(0, S, CH):
                n = min(CH, S - c0)
                scr = qk[64:96, :, :, :].rearrange("p a b c -> p (a b c)")[:, 0:n]

                def ind_a(g, scr=scr, c0=c0, n=n):
                    g.memset(scr, 1.0)
                    return g.affine_select(out=scr, in_=scr, pattern=[[1, n]], compare_op=ALU.is_ge, fill=0.0,
                                           base=c0, channel_multiplier=-256)
                P.op("pool", ind_a, writes=[qkB])
                for hl in range(4):
                    P.op("pool", lambda g, scr=scr, c0=c0, n=n, hl=hl: g.affine_select(
                        out=KT[64:96, hl, c0:c0 + n], in_=scr, pattern=[[-1, n]], compare_op=ALU.is_ge, fill=0.0,
                        base=255 - c0, channel_multiplier=256), reads=[qkB], writes=[KTi])
            P.op("pool", lambda g: g.memset(Vt[:, :, :, 64:65], 1.0), writes=[VoB])
            P.op("pool", lambda g: g.memset(bpad[:, :, :, :], 0.0), writes=[bpB])

            def prep_steps(g, t):
                xs, xsB = xb[t % 2], xbB[t % 2]
                QTt, QTdB, QTbB = QT[t % 2], QTd[t % 2], QTb[t % 2]
                for s in range(4):
                    st_i = t * 4 + s
                    vh = 0
                    b1, b2 = nb(), nb()
                    bq, bqB = banks[b1], bankb[b1]
                    bv, bvB = banks[b2], bankb[b2]
                    fns = [(lambda pe, kc=kc, s=s, bq=bq: pe.matmul(bq[:, :], lhsT=xs[:, kc, s * 128:(s + 1) * 128], rhs=Wq[:, kc, 0:512],
                                                              start=(kc == 0), stop=(kc == 7))) for kc in range(8)]
                    P.op("pe", fns, reads=[xsB, WqB], writes=[bqB])
                    ps3 = bq[:, :].rearrange("p (h d) -> p h d", h=8)
                    cosb = cosT[:, st_i:st_i + 1, :].to_broadcast([128, 8, 8])
                    sinb = sinT[:, st_i:st_i + 1, :].to_broadcast([128, 8, 8])
                    P.op("dve", lambda v, ps3=ps3, s=s: v.tensor_copy(out=qk[:, s, :, 16:64], in_=ps3[:, :, 16:64]),
                         reads=[bqB], writes=[qkB])
                    P.op("dve", lambda v, ps3=ps3, cosb=cosb: v.tensor_tensor(out=rt1[:, :, :], in0=ps3[:, :, 0:8], in1=cosb, op=ALU.mult),
                         reads=[bqB, ropeB], writes=[rtB])
                    P.op("dve", lambda v, ps3=ps3, sinb=sinb: v.tensor_tensor(out=rt2[:, :, :], in0=ps3[:, :, 8:16], in1=sinb, op=ALU.mult),
                         reads=[bqB, ropeB], writes=[rtB])
                    P.op("dve", lambda v, s=s: v.tensor_tensor(out=qk[:, s, :, 0:8], in0=rt1[:, :, :], in1=rt2[:, :, :], op=ALU.subtract),
                         reads=[rtB], writes=[qkB, rtB])
                    P.op("dve", lambda v, ps3=ps3, cosb=cosb: v.tensor_tensor(out=rt1[:, :, :], in0=ps3[:, :, 8:16], in1=cosb, op=ALU.mult),
                         reads=[bqB, ropeB], writes=[rtB])
                    P.op("dve", lambda v, ps3=ps3, sinb=sinb: v.tensor_tensor(out=rt2[:, :, :], in0=ps3[:, :, 0:8], in1=sinb, op=ALU.mult),
                         reads=[bqB, ropeB], writes=[rtB])
                    P.op("dve", lambda v, s=s: v.tensor_tensor(out=qk[:, s, :, 8:16], in0=rt1[:, :, :], in1=rt2[:, :, :], op=ALU.add),
                         reads=[rtB], writes=[qkB, rtB])
                    yield
                    fns = [(lambda pe, kc=kc, s=s, vh=vh, bv=bv: pe.matmul(bv[:, vh:vh + 256], lhsT=xs[:, kc, s * 128:(s + 1) * 128],
                                                                     rhs=Wq[:, kc, 512:768], start=(kc == 0), stop=(kc == 7)))
                           for kc in range(8)]
                    P.op("pe", fns, reads=[xsB, WqB], writes=[bvB])
                    P.op("dve", lambda v, vh=vh, st_i=st_i, bv=bv: v.tensor_copy(
                        out=Vt[:, st_i, :, 0:64], in_=bv[:, vh:vh + 256].rearrange("p (h d) -> p h d", h=4)),
                        reads=[bvB], writes=[VB[t]])
                    yield
                if t + 2 < NT:
                    load_x(t + 2)
                for n_i, idx in enumerate(list(range(4, 8)) + list(range(0, 4))):
                    b = nb()
                    fns = [(lambda pe, s=s, idx=idx, b=b: pe.matmul(banks[b][0:64, s * 128:(s + 1) * 128], lhsT=qk[:, s, idx, :],
                                                                    rhs=ident[:, :], start=True, stop=True)) for s in range(4)]
                    P.op("pe", fns, reads=[qkB, cB], writes=[bankb[b]])
                    if idx >= 4:
                        hl = idx - 4
                        P.op("dve", lambda v, b=b, hl=hl: v.tensor_copy(out=KT[0:64, hl, t * 512:(t + 1) * 512], in_=banks[b][0:64, :]),
                             reads=[bankb[b]], writes=[KTd[t]])

                        def kmred(v, b=b, hl=hl):
                            with nc.allow_low_precision("fp32 accumulate, bf16 store of block key sums"):
                                return v.tensor_reduce(out=KM[:, hl, 2 * t:2 * t + 2],
                                                       in_=banks[b][0:64, :].rearrange("p (j k) -> p j k", j=2), axis=AX.X, op=ALU.add)
                        P.op("dve", kmred, reads=[bankb[b]], writes=[KMB])
                    else:
                        hl = idx
                        P.op("dve", lambda v, b=b, hl=hl: v.tensor_copy(out=QTt[0:64, hl, :], in_=banks[b][0:64, :]),
                             reads=[bankb[b]], writes=[QTdB])
                    yield
                BA = nb()
                fns = [(lambda pe, s=s, hl=hl: pe.matmul(banks[BA][:, (s * 4 + hl) * 32:(s * 4 + hl + 1) * 32],
                                                         lhsT=QTt[0:64, hl, s * 128:(s + 1) * 128], rhs=KM[:, hl, :], start=True, stop=True))
                       for s in range(4) for hl in range(4)]
                P.op("pe", fns, reads=[QTdB, KMB], writes=[bankb[BA]])
                P.op("dve", lambda v: v.tensor_tensor(out=scb[:, :, :], in0=banks[BA][:, :].rearrange("p (a j) -> p a j", j=32),
                                                     in1=maskb[:, :, :, :].rearrange("p s h j -> p (s h) j"), op=ALU.add),
                     reads=[bankb[BA], maskB], writes=[scB])
                for a in range(16):
                    P.op("dve", lambda v, a=a: v.max(out=m8[:, a, :], in_=scb[:, a, :]), reads=[scB], writes=[m8B])
                P.op("dve", lambda v: v.tensor_tensor(out=ltm[:, :, :], in0=scb[:, :, :],
                                                     in1=m8[:, :, 2:3].to_broadcast([128, 16, 32]), op=ALU.is_lt),
                     reads=[scB, m8B], writes=[ltB])
                P.op("dve", lambda v: v.tensor_scalar(out=bpad[:, :, :, 64:96].rearrange("p s h j -> p (s h) j"), in0=ltm[:, :, :],
                                                     scalar1=NEG, scalar2=None, op0=ALU.mult), reads=[ltB], writes=[bpB])

                def own_fix(v):
                    v.memset(bpad[:, 0:2, :, 64 + 2 * t:64 + 2 * t + 1], 0.0)
                    return v.memset(bpad[:, 2:4, :, 64 + 2 * t + 1:64 + 2 * t + 2], 0.0)
                P.op("dve", own_fix, reads=[], writes=[bpB])
                if t + 1 < NT:
                    def mask_upd(gg):
                        gg.memset(maskb[:, 0:2, :, 2 * t:2 * t + 2], 0.0)
                        return gg.memset(maskb[:, 2:4, :, 2 * t + 1:2 * t + 3], 0.0)
                    P.op("pool", mask_upd, writes=[maskB])
                yield
                for hl in range(4):
                    b = nb()
                    fns = [(lambda pe, s=s, hl=hl, b=b: pe.matmul(banks[b][0:96, s * 128:(s + 1) * 128], lhsT=bpad[:, s, hl, :],
                                                                  rhs=ident[:, :], start=True, stop=True)) for s in range(4)]
                    P.op("pe", fns, reads=[bpB, cB], writes=[bankb[b]])
                    P.op("dve", lambda v, b=b, hl=hl: v.tensor_copy(out=QTt[64:96, hl, :], in_=banks[b][64:96, :]),
                         reads=[bankb[b]], writes=[QTbB])
                    yield

            pending = []
            sctr = [0]
            pctr = [0]

            def attention(g, t, prep):
                QTt, QTdB, QTbB = QT[t % 2], QTd[t % 2], QTb[t % 2]
                items = []
                for hl in range(4):
                    for pp in range(2 * t):
                        items.append((hl, [2 * pp, 2 * pp + 1]))
                    for c in range(4):
                        items.append((hl, [4 * t + c]))
                nitems = len(items)
                nprep = 21
                slots = {}

                def emit_S(i):
                    hl, kts = items[i]
                    si = sctr[0] % 2
                    sctr[0] += 1
                    slots[i] = si
                    sp = pairs[SPAIR[si]]
                    fns = []
                    for j, kt in enumerate(kts):
                        c = kt - 4 * t
                        q0 = max(c, 0) * 128
                        fns.append(lambda pe, j=j, kt=kt, q0=q0, c=c, hl=hl: pe.matmul(
                            sp[:, j * 512 + q0:(j + 1) * 512], lhsT=KT[0:96, hl, kt * 128:(kt + 1) * 128], rhs=QTt[0:96, hl, q0:512],
                            start=True, stop=(c < 0)))
                        if c >= 0:
                            fns.append(lambda pe, j=j, q0=q0: pe.matmul(
                                sp[:, j * 512 + q0:j * 512 + q0 + 128], lhsT=ident[:, :], rhs=tri[:, :], start=False, stop=True))
                    rd = [KTi, QTdB, QTbB, cB] + [KTd[kt // 4] for kt in kts]
                    P.op("pe", fns, reads=rd, writes=[spB[si]])

                pis = {}

                def emit_E(i):
                    hl, kts = items[i]
                    si = slots.pop(i)
                    sp = pairs[SPAIR[si]]
                    pi = pctr[0] % NPT
                    pctr[0] += 1
                    pis[i] = pi
                    c = kts[0] - 4 * t
                    q0 = max(c, 0) * 128
                    w = len(kts) * 512
                    P.op("act", lambda a: a.activation(out=pts[pi][:, q0:w], in_=sp[:, q0:w], func=AF.Exp, scale=0.125),
                         reads=[spB[si]], writes=[ptB[pi]])

                def emit_P(i):
                    hl, kts = items[i]
                    pi = pis.pop(i)
                    ob = OB
                    c = kts[0] - 4 * t
                    q0 = max(c, 0) * 128
                    nkt = 4 * t + 4
                    for j, kt in enumerate(kts):
                        P.op("pe", lambda pe, j=j, kt=kt: pe.matmul(
                            banks[ob][0:65, q0:512], lhsT=Vt[:, kt, hl, :], rhs=pts[pi][:, j * 512 + q0:(j + 1) * 512],
                            start=(kt == 0), stop=(kt == nkt - 1)),
                            reads=[VB[kt // 4], VoB, ptB[pi]], writes=[bankb[ob]], acc=(kt > 0))
                    if kts[-1] == nkt - 1:
                        def norm(hl=hl, ob=ob):
                            obB = bankb[ob]
                            P.op("dve", lambda v: v.tensor_copy(out=osb[:, :], in_=banks[ob][0:65, :]), reads=[obB], writes=[osbB])
                            BB = nb()
                            P.op("pe", lambda pe: pe.matmul(banks[BB][0:64, :], lhsT=onesf[64:65, 0:64], rhs=osb[64:65, :],
                                                            start=True, stop=True), reads=[osbB, cB], writes=[bankb[BB]])
                            P.op("dve", lambda v: v.reciprocal(out=rec[:, :], in_=banks[BB][0:64, :]), reads=[bankb[BB]], writes=[recB])
                            oi = hl % 2
                            P.op("dve", lambda v: v.tensor_tensor(out=oT[oi][:, :], in0=osb[0:64, :], in1=rec[:, :], op=ALU.mult),
                                 reads=[osbB, recB], writes=[oTB[oi]])
                            hg = g * 4 + hl
                            P.dma("sp", [(oscr[hg * 64:(hg + 1) * 64, t * 512:(t + 1) * 512], oT[oi][:, :])], s_o[oi], reads=[oTB[oi]])
                        pending.append(norm)

                emit_S(0)
                if nitems > 1:
                    emit_S(1)
                done_prep = 0
                for i in range(nitems):
                    emit_E(i)
                    if i + 2 < nitems:
                        emit_S(i + 2)
                    while pending:
                        pending.pop(0)()
                    emit_P(i)
                    if prep is not None:
                        target = min(nprep, (10 * (i + 1) * nprep + 5 * nitems - 1) // (5 * nitems))
                        while done_prep < target:
                            try:
                                next(prep)
                            except StopIteration:
                                prep = None
                                break
                            done_prep += 1
                if prep is not None:
                    for _ in prep:
                        pass

            for g in range(2):
                if g > 0:
                    load_w1(g)
                    load_x(0)
                    if NT > 1:
                        load_x(1)
                P.op("pool", lambda gg: gg.memset(KM[:, :, :], 0.0), writes=[KMB])

                def mask_init(gg):
                    gg.memset(maskb[:, :, :, :], -1e30)
                    return gg.memset(maskb[:, 2:4, :, 0:1], 0.0)
                P.op("pool", mask_init, writes=[maskB])
                for _ in prep_steps(g, 0):
                    pass
                for t in range(NT):
                    attention(g, t, prep_steps(g, t + 1) if t + 1 < NT else None)
                while pending:
                    pending.pop(0)()
            P.emit([(s_o[0], s_o[0].n), (s_o[1], s_o[1].n)])

        if stop_after == 1:
            return nc

        bg_sb = sb("bg_sb", [128, 16], F32)
        cwm_sb = sb("cwm_sb", [128, 4, 3], F32)
        ln1_sb = sb("ln1_sb", [128, 2, 8], F32)
        cwf_sb = sb("cwf_sb", [128, 44, 3], F32)
        ln2_sb = sb("ln2_sb", [128, 2, 8], F32)
        s_vec = P.new_sem("s_vec")
        vecB = Buf("vec")
        P.dma("sp", [(bg_sb[:, :], bgl), (cwm_sb[:, :, :], cwm), (ln1_sb[:, :, :], ln1), (cwf_sb[:, :, :], cwf),
                     (ln2_sb[:, :, :], ln2)], s_vec, writes=[vecB])

        def rearr_w(w):
            return w.rearrange("(k p) n -> p k n", p=128)

        class Ring:
            def __init__(self, idxs):
                self.idxs = idxs
                self.p = 0

            def nxt(self):
                b = self.idxs[self.p % len(self.idxs)]
                self.p += 1
                return b

        def layer_norm_tile(r, rB, S1, S2, gb_sb, stA, stB, stBuf, out_rows, t, s_out):
            S1b, S2b = bankb[S1], bankb[S2]
            P.op("act", lambda a: a.activation(out=stA[:, :], in_=banks[S1][:, :], func=AF.Copy, scale=1.0 / D),
                 reads=[S1b], writes=[stBuf[0]])
            yield
            P.op("dve", lambda v: v.tensor_tensor(out=stB[:, :], in0=stA[:, :], in1=stA[:, :], op=ALU.mult),
                 reads=[stBuf[0]], writes=[stBuf[1]])
            yield
            P.op("dve", lambda v: v.scalar_tensor_tensor(out=stB[:, :], in0=banks[S2][:, :], scalar=1.0 / D, in1=stB[:, :],
                                                          op0=ALU.mult, op1=ALU.subtract),
                 reads=[S2b, stBuf[1]], writes=[stBuf[1]])
            yield
            P.op("act", lambda a: a.activation(out=stB[:, :], in_=stB[:, :], func=AF.Ln, bias=epsc[:, 0:1]),
                 reads=[stBuf[1], cB], writes=[stBuf[1]])
            yield
            P.op("act", lambda a: a.activation(out=stB[:, :], in_=stB[:, :], func=AF.Exp, scale=-0.5),
                 reads=[stBuf[1]], writes=[stBuf[1]])
            yield

            def sub(m):
                P.op("pool", lambda v: v.tensor_tensor(out=r[:, m, :], in0=r[:, m, :], in1=stA[:, :], op=ALU.subtract),
                     reads=[stBuf[0], rB[m]], writes=[rB[m]])

            def mul(m):
                P.op("dve", lambda v: v.tensor_tensor(out=r[:, m, :], in0=r[:, m, :], in1=stB[:, :], op=ALU.mult),
                     reads=[stBuf[1], rB[m]], writes=[rB[m]])

            def aff(m):
                P.op("act", lambda a: a.activation(out=r[:, m, :], in_=r[:, m, :], func=AF.Identity,
                                                   bias=gb_sb[:, 1, m:m + 1], scale=gb_sb[:, 0, m:m + 1]),
                     reads=[rB[m], vecB], writes=[rB[m]])

            def out(m):
                P.dma("sp", [(out_rows[m * 128:(m + 1) * 128, t * 512:(t + 1) * 512], r[:, m, :])], s_out[m % len(s_out)],
                      reads=[rB[m]])
            for k in range(11):
                if k < 8:
                    sub(k)
                if 0 <= k - 1 < 8:
                    mul(k - 1)
                if 0 <= k - 2 < 8:
                    aff(k - 2)
                if 0 <= k - 3 < 8:
                    out(k - 3)
                yield

        ln_gen = [None]

        def ln_step():
            if ln_gen[0] is not None:
                try:
                    next(ln_gen[0])
                except StopIteration:
                    ln_gen[0] = None

        def ln_drain():
            while ln_gen[0] is not None:
                ln_step()

        def ln_stats_chunk(r, rB, m, S1, S2, rbq, rbqB, k):
            rb_t, sq_t = rbq[k % 2]
            rbB, sqB = rbqB[k % 2]
            P.op("act", lambda a: a.activation(out=rb_t[:, :], in_=r[:, m, :], func=AF.Copy), reads=[rB[m]], writes=[rbB])
            P.op("act", lambda a: a.activation(out=sq_t[:, :], in_=r[:, m, :], func=AF.Square), reads=[rB[m]], writes=[sqB])

            def pe_part():
                P.op("pe", lambda pe: pe.matmul(banks[S1][:, :], lhsT=onesb[:, :], rhs=rb_t[:, :], start=(m == 0), stop=(m == 7)),
                     reads=[rbB, cB], writes=[bankb[S1]], acc=(m > 0))
                P.op("pe", lambda pe: pe.matmul(banks[S2][:, :], lhsT=onesb[:, :], rhs=sq_t[:, :], start=(m == 0), stop=(m == 7)),
                     reads=[sqB, cB], writes=[bankb[S2]], acc=(m > 0))
            return pe_part

        with ExitStack() as st2:
            sb2 = lambda name, shape, dt: st2.enter_context(nc.sbuf_tensor(name, shape, dt))
            W2 = sb2("W2", [128, 8, 3584], BF16)
            Wat = sb2("Wat", [128, 4, 1024], BF16)
            Wcv = sb2("Wcv", [128, 4, 1024], BF16)
            Wo = sb2("Wo", [128, 8, 1024], BF16)
            xb2 = [sb2("xb2_%d" % i, [128, 8, 512], BF16) for i in range(2)]
            xfr = [sb2("xfr%d" % i, [128, 512], F32) for i in range(3)]
            oTt = [sb2("oTt%d" % i, [128, 4, 512], BF16) for i in range(2)]
            CHt = [sb2("CH%d" % i, [128, 514], F32) for i in range(2)]
            halo = sb2("halo", [128, 4, 2], F32)
            hS = [sb2("hS%d" % i, [128, 512], F32) for i in range(2)]
            acc = [sb2("acc%d" % i, [128, 512], F32) for i in range(2)]
            cb = sb2("cb", [128, 4, 512], BF16)
            ga = [sb2("ga%d" % i, [128, 512], F32) for i in range(2)]
            gc = [sb2("gc%d" % i, [128, 512], F32) for i in range(2)]
            t1 = [sb2("t1_%d" % i, [128, 512], F32) for i in range(2)]
            t2b = [sb2("t2_%d" % i, [128, 512], F32) for i in range(2)]
            mixin = sb2("mixin", [128, 8, 512], BF16)
            r2 = sb2("r2", [128, 8, 512], F32)
            rbq = [(sb2("rb%d" % i, [128, 512], BF16), sb2("sq%d" % i, [128, 512], BF16)) for i in range(2)]
            stA = sb2("stA", [128, 512], F32)
            stB = sb2("stB", [128, 512], F32)

            s_w2 = [P.new_sem("s_w2_%d" % i) for i in range(10)]
            s_x2 = [P.new_sem("s_x2_%d" % i) for i in range(2)]
            s_xf = [P.new_sem("s_xf%d" % i) for i in range(3)]
            s_ot = [P.new_sem("s_ot%d" % i) for i in range(2)]
            s_x1 = [P.new_sem("s_x1_%d" % i) for i in range(8)]
            W2B = [Buf("W2_%d" % i) for i in range(7)]
            WatB, WcvB, WoB = Buf("Wat"), Buf("Wcv"), Buf("Wo")
            xb2B = [Buf(), Buf()]
            xfrB = [Buf(), Buf(), Buf()]
            oTtB = [Buf(), Buf()]
            CHB = [Buf(), Buf()]
            haloB = Buf()
            hSB = [Buf(), Buf()]
            accB = [Buf(), Buf()]
            cbB = [Buf() for _ in range(4)]
            gaB, gcB, t1B, t2B2 = [Buf(), Buf()], [Buf(), Buf()], [Buf(), Buf()], [Buf(), Buf()]
            mixB = [Buf() for _ in range(8)]
            r2B = [Buf() for _ in range(8)]
            rbqB = [(Buf(), Buf()), (Buf(), Buf())]
            stBuf2 = [Buf(), Buf()]

            for j in range(3):
                P.dma("pool", [(W2[:, :, j * 512:(j + 1) * 512], rearr_w(w_in)[:, :, 1536 + j * 512:1536 + (j + 1) * 512])],
                      s_w2[j], writes=[W2B[j]])
            P.dma("pool", [(Wcv[:, :, :], rearr_w(w_cnv))], s_w2[7], writes=[WcvB])
            P.dma("pool", [(Wat[:, :, :], rearr_w(w_att))], s_w2[8], writes=[WatB])
            for j in range(3, 7):
                P.dma("pool", [(W2[:, :, j * 512:(j + 1) * 512], rearr_w(w_in)[:, :, 1536 + j * 512:1536 + (j + 1) * 512])],
                      s_w2[j], writes=[W2B[j]])
            P.dma("pool", [(Wo[:, :, :], rearr_w(w_o))], s_w2[9], writes=[WoB])
            P.op("pool", lambda gg: gg.memset(halo[:, :, :], 0.0), writes=[haloB])

            def load2(t):
                P.dma("pool", [(xb2[t % 2][:, :, :], rearr_w(xT)[:, :, t * 512:(t + 1) * 512])], s_x2[t % 2],
                      writes=[xb2B[t % 2]])
                P.dma("sp", [(oTt[t % 2][:, :, :], rearr_w(oscr)[:, :, t * 512:(t + 1) * 512])], s_ot[t % 2],
                      writes=[oTtB[t % 2]])

            ring2 = Ring([0, 1, 2, 3, 4, 5])
            xfk = [0]

            def p2_tile(t):
                if t + 1 < NT:
                    load2(t + 1)
                xs, xsB = xb2[t % 2], xb2B[t % 2]
                ot, otB = oTt[t % 2], oTtB[t % 2]

                def proj(col0, wB):
                    b = ring2.nxt()
                    fns = [(lambda pe, kc=kc: pe.matmul(banks[b][:, :], lhsT=W2[:, kc, col0:col0 + 128], rhs=xs[:, kc, :],
                                                        start=(kc == 0), stop=(kc == 7))) for kc in range(8)]
                    P.op("pe", fns, reads=[xsB, wB], writes=[bankb[b]])
                    return b

                for cc in range(4):
                    k = cc % 2
                    bH = proj(1024 + cc * 128, W2B[2])
                    bC = proj(512 + cc * 128, W2B[1])
                    bB = proj(cc * 128, W2B[0])
                    P.op("act", lambda a, bH=bH, k=k: a.activation(out=hS[k][:, :], in_=banks[bH][:, :], func=AF.Copy),
                         reads=[bankb[bH]], writes=[hSB[k]])
                    P.op("pool", lambda gg, k=k, cc=cc: gg.tensor_copy(out=CHt[k][:, 0:2], in_=halo[:, cc, :]),
                         reads=[haloB], writes=[CHB[k]])
                    P.op("dve", lambda v, bC=bC, k=k: v.tensor_tensor(out=CHt[k][:, 2:514], in0=banks[bC][:, :], in1=hS[k][:, :], op=ALU.mult),
                         reads=[bankb[bC], hSB[k]], writes=[CHB[k]], extra=CHB[k].w)
                    P.op("act", lambda a, k=k, cc=cc: a.activation(out=acc[k][:, :], in_=CHt[k][:, 2:514], func=AF.Copy,
                                                                   scale=cwm_sb[:, cc, 2:3]),
                         reads=[CHB[k], vecB], writes=[accB[k]])
                    P.op("dve", lambda v, k=k, cc=cc: v.scalar_tensor_tensor(
                        out=acc[k][:, :], in0=CHt[k][:, 1:513], scalar=cwm_sb[:, cc, 1:2], in1=acc[k][:, :], op0=ALU.mult, op1=ALU.add),
                        reads=[CHB[k], vecB, accB[k]], writes=[accB[k]])
                    P.op("dve", lambda v, k=k, cc=cc: v.scalar_tensor_tensor(
                        out=acc[k][:, :], in0=CHt[k][:, 0:512], scalar=cwm_sb[:, cc, 0:1], in1=acc[k][:, :], op0=ALU.mult, op1=ALU.add),
                        reads=[CHB[k], vecB, accB[k]], writes=[accB[k]])
                    P.op("pool", lambda gg, k=k, cc=cc: gg.tensor_copy(out=halo[:, cc, :], in_=CHt[k][:, 512:514]),
                         reads=[CHB[k]], writes=[haloB])
                    P.op("dve", lambda v, bB=bB, k=k, cc=cc: v.tensor_tensor(out=cb[:, cc, :], in0=banks[bB][:, :], in1=acc[k][:, :], op=ALU.mult),
                         reads=[bankb[bB], accB[k]], writes=[cbB[cc]])
                    ln_step()
                    ln_step()
                for m in range(8):
                    k = m % 2
                    bGa = proj(1536 + m * 128, W2B[3 + m // 4])
                    bGc = proj(2560 + m * 128, W2B[5 + m // 4])
                    bYa = ring2.nxt()
                    fns = [(lambda pe, kc=kc, m=m, bYa=bYa: pe.matmul(banks[bYa][:, :], lhsT=Wat[:, kc, m * 128:(m + 1) * 128], rhs=ot[:, kc, :],
                                                        start=(kc == 0), stop=(kc == 3))) for kc in range(4)]
                    P.op("pe", fns, reads=[otB, WatB], writes=[bankb[bYa]])
                    bYc = ring2.nxt()
                    fns = [(lambda pe, kc=kc, m=m, bYc=bYc: pe.matmul(banks[bYc][:, :], lhsT=Wcv[:, kc, m * 128:(m + 1) * 128], rhs=cb[:, kc, :],
                                                        start=(kc == 0), stop=(kc == 3))) for kc in range(4)]
                    P.op("pe", fns, reads=cbB + [WcvB], writes=[bankb[bYc]])
                    P.op("act", lambda a, bGa=bGa, k=k, m=m: a.activation(out=ga[k][:, :], in_=banks[bGa][:, :], func=AF.Sigmoid,
                                                                          bias=bg_sb[:, m:m + 1]),
                         reads=[bankb[bGa], vecB], writes=[gaB[k]])
                    P.op("act", lambda a, bGc=bGc, k=k, m=m: a.activation(out=gc[k][:, :], in_=banks[bGc][:, :], func=AF.Sigmoid,
                                                                          bias=bg_sb[:, 8 + m:9 + m]),
                         reads=[bankb[bGc], vecB], writes=[gcB[k]])
                    P.op("dve", lambda v, bYa=bYa, k=k: v.tensor_tensor(out=t1[k][:, :], in0=banks[bYa][:, :], in1=ga[k][:, :], op=ALU.mult),
                         reads=[bankb[bYa], gaB[k]], writes=[t1B[k]])
                    P.op("dve", lambda v, bYc=bYc, k=k: v.tensor_tensor(out=t2b[k][:, :], in0=banks[bYc][:, :], in1=gc[k][:, :], op=ALU.mult),
                         reads=[bankb[bYc], gcB[k]], writes=[t2B2[k]])
                    P.op("pool", lambda gg, k=k, m=m: gg.tensor_tensor(out=mixin[:, m, :], in0=t1[k][:, :], in1=t2b[k][:, :], op=ALU.add),
                         reads=[t1B[k], t2B2[k]], writes=[mixB[m]])
                    ln_step()
                    ln_step()
                ln_drain()
                pend = [None]
                NF2 = 6
                obank = {}
                for m in range(6):
                    b = ring2.nxt()
                    obank[m] = b
                    fns = [(lambda pe, kc=kc, m=m, b=b: pe.matmul(banks[b][:, :], lhsT=Wo[:, kc, m * 128:(m + 1) * 128], rhs=mixin[:, kc, :],
                                                        start=(kc == 0), stop=False)) for kc in range(NF2)]
                    P.op("pe", fns, reads=mixB[:NF2] + [WoB], writes=[bankb[b]])
                for m in range(8):
                    xi = xfk[0] % 3
                    xfk[0] += 1
                    P.dma("sp", [(xfr[xi][:, :], xT[m * 128:(m + 1) * 128, t * 512:(t + 1) * 512])], s_xf[xi], writes=[xfrB[xi]])
                    if m < 6:
                        b = obank[m]
                        fns = [(lambda pe, kc=kc, m=m, b=b: pe.matmul(banks[b][:, :], lhsT=Wo[:, kc, m * 128:(m + 1) * 128], rhs=mixin[:, kc, :],
                                                            start=False, stop=(kc == 7))) for kc in range(NF2, 8)]
                        P.op("pe", fns, reads=mixB + [WoB], writes=[bankb[b]], acc=True)
                    else:
                        b = ring2.nxt()
                        fns = [(lambda pe, kc=kc, m=m, b=b: pe.matmul(banks[b][:, :], lhsT=Wo[:, kc, m * 128:(m + 1) * 128], rhs=mixin[:, kc, :],
                                                            start=(kc == 0), stop=(kc == 7))) for kc in range(8)]
                        P.op("pe", fns, reads=mixB + [WoB], writes=[bankb[b]])
                    P.op("dve", lambda v, b=b, xi=xi, m=m: v.scalar_tensor_tensor(
                        out=r2[:, m, :], in0=xfr[xi][:, :], scalar=ALPHA, in1=banks[b][:, :], op0=ALU.mult, op1=ALU.add),
                        reads=[xfrB[xi], bankb[b]], writes=[r2B[m]])
                    if pend[0] is not None:
                        pend[0]()
                    pend[0] = ln_stats_chunk(r2, r2B, m, 6, 7, rbq, rbqB, m)
                pend[0]()
                pend[0] = None
                ln_gen[0] = layer_norm_tile(r2, r2B, 6, 7, ln1_sb, stA, stB, stBuf2, x1scr, t, s_x1)

            load2(0)
            for t in range(NT):
                p2_tile(t)
            ln_drain()
            P.emit([(s, s.n) for s in s_x1])

        if stop_after == 2:
            return nc

        with ExitStack() as st3:
            sb3 = lambda name, shape, dt: st3.enter_context(nc.sbuf_tensor(name, shape, dt))
            Wup = sb3("Wup", [128, 8, 5632], BF16)
            Wdn = sb3("Wdn", [128, 22, 1024], BF16)
            x1b = sb3("x1b", [128, 8, 512], BF16)
            x1r = [sb3("x1r%d" % i, [128, 512], F32) for i in range(3)]
            hid = sb3("hid", [128, 22, 512], BF16)
            accg = [sb3("accg%d" % i, [128, 512], F32) for i in range(2)]
            accv = [sb3("accv%d" % i, [128, 512], F32) for i in range(2)]
            ylast = [sb3("ylast%d" % i, [128, 44, 2], F32) for i in range(2)]
            corr = sb3("corr", [128, 44, 2], F32)
            ctmp = sb3("ctmp", [128, 44], F32)
            r3 = sb3("r3", [128, 8, 512], F32)
            rbq3 = [(sb3("rb3_%d" % i, [128, 512], BF16), sb3("sq3_%d" % i, [128, 512], BF16)) for i in range(2)]
            stA3 = sb3("stA3", [128, 512], F32)
            stB3 = sb3("stB3", [128, 512], F32)

            s_w3 = [P.new_sem("s_w3_%d" % i) for i in range(14)]
            s_x3 = P.new_sem("s_x3")
            s_xr = [P.new_sem("s_xr%d" % i) for i in range(3)]
            s_y = [P.new_sem("s_y%d" % i) for i in range(8)]
            WupB = [Buf() for _ in range(11)]
            WdnB = [Buf() for _ in range(3)]
            x1bB = Buf()
            x1rB = [Buf(), Buf(), Buf()]
            hidB = [Buf() for _ in range(22)]
            accgB, accvB = [Buf(), Buf()], [Buf(), Buf()]
            ylB = [[Buf() for _ in range(44)] for _ in range(2)]
            corrB = Buf()
            r3B = [Buf() for _ in range(8)]
            rbq3B = [(Buf(), Buf()), (Buf(), Buf())]
            stBuf3 = [Buf(), Buf()]

            order = []
            for j in range(11):
                order.append(j)
            seq = [0, 5, 6, 1, 7, 2, 8, 3, 9, 4, 10]
            for j in seq:
                P.dma("pool", [(Wup[:, :, j * 512:(j + 1) * 512], rearr_w(w_up)[:, :, j * 512:(j + 1) * 512])], s_w3[j],
                      writes=[WupB[j]])
            dsp = [(0, 8), (8, 16), (16, 22)]
            for j, (a0, a1) in enumerate(dsp):
                P.dma("pool", [(Wdn[:, a0:a1, :], rearr_w(w_dn)[:, a0:a1, :])], s_w3[11 + j], writes=[WdnB[j]])
            P.op("pool", lambda gg: gg.memset(ylast[0][:, :, :], 0.0), writes=ylB[0])

            def load3(t):
                P.dma("pool", [(x1b[:, :, :], rearr_w(x1scr)[:, :, t * 512:(t + 1) * 512])], s_x3, writes=[x1bB])

            ring3 = Ring([0, 1, 2, 3, 4, 5])
            xrk = [0]

            def wB_for(col):
                return WupB[col // 512]

            def p3_tile(t):
                def up(col0):
                    b = ring3.nxt()
                    fns = [(lambda pe, kc=kc: pe.matmul(banks[b][:, :], lhsT=Wup[:, kc, col0:col0 + 128], rhs=x1b[:, kc, :],
                                                        start=(kc == 0), stop=(kc == 7))) for kc in range(8)]
                    P.op("pe", fns, reads=[x1bB, wB_for(col0)], writes=[bankb[b]])
                    return b

                yl_cur, ylB_cur = ylast[t % 2], ylB[t % 2]
                yl_nxt, ylB_nxt = ylast[(t + 1) % 2], ylB[(t + 1) % 2]
                P.op("pool", lambda gg: gg.tensor_tensor(out=corr[:, :, 0], in0=yl_cur[:, :, 0], in1=cwf_sb[:, :, 0], op=ALU.mult),
                     reads=ylB_cur + [vecB], writes=[corrB])
                P.op("pool", lambda gg: gg.tensor_tensor(out=ctmp[:, :], in0=yl_cur[:, :, 1], in1=cwf_sb[:, :, 1], op=ALU.mult),
                     reads=ylB_cur + [vecB], writes=[corrB])
                P.op("pool", lambda gg: gg.tensor_tensor(out=corr[:, :, 0], in0=corr[:, :, 0], in1=ctmp[:, :], op=ALU.add),
                     reads=[corrB], writes=[corrB])
                P.op("pool", lambda gg: gg.tensor_tensor(out=corr[:, :, 1], in0=yl_cur[:, :, 1], in1=cwf_sb[:, :, 0], op=ALU.mult),
                     reads=ylB_cur + [vecB], writes=[corrB])

                def conv(b, ch, a_t, aB):
                    w0, w1, w2 = cwf_sb[:, ch, 0:1], cwf_sb[:, ch, 1:2], cwf_sb[:, ch, 2:3]
                    P.op("act", lambda a: a.activation(out=a_t[:, :], in_=banks[b][:, :], func=AF.Copy, scale=w2),
                         reads=[bankb[b], vecB], writes=[aB])
                    P.op("act", lambda a: a.activation(out=yl_nxt[:, ch, :], in_=banks[b][:, 510:512], func=AF.Copy),
                         reads=[bankb[b]], writes=[ylB_nxt[ch]])
                    P.op("dve", lambda v: v.scalar_tensor_tensor(out=a_t[:, 1:512], in0=banks[b][:, 0:511], scalar=w1, in1=a_t[:, 1:512],
                                                                  op0=ALU.mult, op1=ALU.add),
                         reads=[bankb[b], vecB, aB], writes=[aB])
                    P.op("dve", lambda v: v.scalar_tensor_tensor(out=a_t[:, 2:512], in0=banks[b][:, 0:510], scalar=w0, in1=a_t[:, 2:512],
                                                                  op0=ALU.mult, op1=ALU.add),
                         reads=[bankb[b], vecB, aB], writes=[aB])
                    P.op("pool", lambda gg: gg.tensor_tensor(out=a_t[:, 0:2], in0=a_t[:, 0:2], in1=corr[:, ch, :], op=ALU.add),
                         reads=[corrB, aB], writes=[aB])

                for i in range(22):
                    k = i % 2
                    bg_ = up(i * 128)
                    bv_ = up(DFF + i * 128)
                    conv(bg_, i, accg[k], accgB[k])
                    conv(bv_, 22 + i, accv[k], accvB[k])
                    P.op("act", lambda a, k=k: a.activation(out=accg[k][:, :], in_=accg[k][:, :], func=AF.Silu),
                         reads=[accgB[k]], writes=[accgB[k]])
                    P.op("pool", lambda gg, k=k, i=i: gg.tensor_tensor(out=hid[:, i, :], in0=accg[k][:, :], in1=accv[k][:, :], op=ALU.mult),
                         reads=[accgB[k], accvB[k]], writes=[hidB[i]])
                    ln_step()
                ln_drain()
                if t + 1 < NT:
                    load3(t + 1)
                pend = [None]
                NF = 16
                dbank = {}
                for m in range(6):
                    b = ring3.nxt()
                    dbank[m] = b
                    fns = [(lambda pe, kc=kc, m=m, b=b: pe.matmul(banks[b][:, :], lhsT=Wdn[:, kc, m * 128:(m + 1) * 128], rhs=hid[:, kc, :],
                                                        start=(kc == 0), stop=False)) for kc in range(NF)]
                    P.op("pe", fns, reads=hidB[:NF] + WdnB, writes=[bankb[b]])
                for m in range(8):
                    xi = xrk[0] % 3
                    xrk[0] += 1
                    P.dma("sp", [(x1r[xi][:, :], x1scr[m * 128:(m + 1) * 128, t * 512:(t + 1) * 512])], s_xr[xi], writes=[x1rB[xi]])
                    if m < 6:
                        b = dbank[m]
                        fns = [(lambda pe, kc=kc, m=m, b=b: pe.matmul(banks[b][:, :], lhsT=Wdn[:, kc, m * 128:(m + 1) * 128], rhs=hid[:, kc, :],
                                                            start=False, stop=(kc == 21))) for kc in range(NF, 22)]
                        P.op("pe", fns, reads=hidB + WdnB, writes=[bankb[b]], acc=True)
                    else:
                        b = ring3.nxt()
                        fns = [(lambda pe, kc=kc, m=m, b=b: pe.matmul(banks[b][:, :], lhsT=Wdn[:, kc, m * 128:(m + 1) * 128], rhs=hid[:, kc, :],
                                                            start=(kc == 0), stop=(kc == 21))) for kc in range(22)]
                        P.op("pe", fns, reads=hidB + WdnB, writes=[bankb[b]])
                    P.op("dve", lambda v, b=b, xi=xi, m=m: v.scalar_tensor_tensor(
                        out=r3[:, m, :], in0=x1r[xi][:, :], scalar=ALPHA, in1=banks[b][:, :], op0=ALU.mult, op1=ALU.add),
                        reads=[x1rB[xi], bankb[b]], writes=[r3B[m]])
                    if pend[0] is not None:
                        pend[0]()
                    pend[0] = ln_stats_chunk(r3, r3B, m, 6, 7, rbq3, rbq3B, m)
                pend[0]()
                pend[0] = None
                ln_gen[0] = layer_norm_tile(r3, r3B, 6, 7, ln2_sb, stA3, stB3, stBuf3, yT, t, s_y)

            load3(0)
            for t in range(NT):
                p3_tile(t)
            ln_drain()
            P.emit([(s, s.n) for s in s_y])
    return nc


def _layout_inputs(inputs, b, S):
    x = np.asarray(inputs["x"])[b]
    pos = np.asarray(inputs["positions"])[b].astype(np.int32)
    f = lambda a: np.ascontiguousarray(np.asarray(a, dtype=np.float32))
    m = {
        "xT": f(x.T),
        "posl": np.ascontiguousarray(pos.reshape(S // 128, 128).T),
        "w_in": f(inputs["w_in"][0]),
        "bgl": f(np.asarray(inputs["b_gate"][0]).reshape(16, 128).T),
        "w_att": f(inputs["w_attn_out"][0]),
        "cwm": f(np.asarray(inputs["conv_w_mix"][0]).reshape(3, 4, 128).transpose(2, 1, 0)),
        "w_cnv": f(inputs["w_conv_out"][0]),
        "w_o": f(inputs["w_o"][0]),
        "ln1": f(np.stack([np.asarray(inputs["ln1_g"][0]).reshape(8, 128).T, np.asarray(inputs["ln1_b"][0]).reshape(8, 128).T], axis=1)),
        "w_up": f(inputs["w_up"][0]),
        "cwf": f(np.asarray(inputs["conv_w_ffn"][0]).reshape(3, 44, 128).transpose(2, 1, 0)),
        "w_dn": f(inputs["w_down"][0]),
        "ln2": f(np.stack([np.asarray(inputs["ln2_g"][0]).reshape(8, 128).T, np.asarray(inputs["ln2_b"][0]).reshape(8, 128).T], axis=1)),
    }
    return m


def kernel(**inputs):
    x = np.asarray(inputs["x"])
    B, S, _ = x.shape
    nc = build(S)
    in_maps = [_layout_inputs(inputs, b, S) for b in range(B)]
    res = run_bass_kernel_spmd(nc, in_maps, core_ids=list(range(B)))
    out = np.stack([np.asarray(res.results[b]["yT"]).T for b in range(B)], axis=0)
    return np.ascontiguousarray(out.astype(np.float32))
```
